# Optimizing a Trainium2 kernel written in Bass

```python
import math
import jax, jax.numpy as jnp
from jax import lax
import numpy as np

D_MODEL = 1024
BATCH = 8
SEQ = 2048
DEPTH = 2

RWKV_HEADS = 8
RWKV_HD = 64
RWKV_W = RWKV_HEADS * RWKV_HD
DECAY_LORA = 64
ICLR_LORA = 64
VRES_LORA = 32
GATE_LORA = 128
LNX_EPS = 64e-5
SB_HEADS = 8
SB_HD = 64
SB_W = SB_HEADS * SB_HD
SB_BLOCK = 128
MEM_LEN = 256
MEM_HEADS = 4
MEM_HD = 128
MEM_W = MEM_HEADS * MEM_HD
N_BRANCH = 3
BRANCH_W = 512
D_FF = 2816
CONV_W = 3
NORM_EPS = 1e-6

RWKV_SPLITS = (RWKV_W, RWKV_W, RWKV_W, DECAY_LORA, ICLR_LORA, GATE_LORA)
RWKV_COLS = sum(RWKV_SPLITS)
IN_COLS = RWKV_COLS + 3 * SB_W + MEM_W + N_BRANCH * D_MODEL

kernel_name = "hybrid_rwkv7_stickbreak_memxattn_convffn"


def rms_norm(x, g, eps=NORM_EPS):
    xf = x.astype(jnp.float32)
    y = xf * lax.rsqrt(jnp.mean(xf * xf, axis=-1, keepdims=True) + eps)
    return (y * g.astype(jnp.float32)).astype(x.dtype)


def split_heads(t, h, d):
    return t.reshape(t.shape[:-1] + (h, d))


def token_shift(p, mu):
    prev = jnp.pad(p, ((0, 0), (1, 0), (0, 0)))[:, :-1]
    return p + (prev - p) * mu


def rwkv7_recurrence(r, decay, k, v, kk, a):
    b, t, h, n = r.shape
    xs = tuple(jnp.moveaxis(z, 1, 0) for z in (r, decay, k, v, kk, a))

    def step(S, inp):
        r_t, w_t, k_t, v_t, kk_t, a_t = inp
        sa = jnp.einsum('bhij,bhj->bhi', S, -kk_t)
        S = (S * w_t[:, :, None, :]
             + sa[..., None] * (kk_t * a_t)[:, :, None, :]
             + v_t[..., None] * k_t[:, :, None, :])
        y_t = jnp.einsum('bhij,bhj->bhi', S, r_t)
        return S, y_t

    S0 = jnp.zeros((b, h, n, n), jnp.float32)
    _, ys = lax.scan(step, S0, xs)
    return jnp.moveaxis(ys, 0, 1)


def rwkv7_branch(p, mu, w0, w2, a0, a2, g2, k_k, k_a, r_k, lnx_g, lnx_b, v_first, vres):
    dt = p.dtype
    p = token_shift(p, mu)
    idx = list(np.cumsum(RWKV_SPLITS)[:-1])
    r, k, v, wl, al, gl = jnp.split(p, idx, axis=-1)
    f32 = jnp.float32
    w = -jax.nn.softplus(-(w0 + jnp.tanh(wl) @ w2).astype(f32)) - 0.5
    decay = jnp.exp(-jnp.exp(w))
    a = jax.nn.sigmoid((a0 + al @ a2).astype(f32))
    g = jax.nn.sigmoid(gl) @ g2
    if vres is None:
        v_first = v
    else:
        v0, v1, v2 = vres
        v = v + (v_first - v) * jax.nn.sigmoid(v0 + (v @ v1) @ v2)
    kk = split_heads((k * k_k).astype(f32), RWKV_HEADS, RWKV_HD)
    kk = kk / jnp.maximum(jnp.sqrt(jnp.sum(kk * kk, axis=-1, keepdims=True)), 1e-12)
    k = k.astype(f32) * (1.0 + (a - 1.0) * k_a.astype(f32))
    rh = split_heads(r.astype(f32), RWKV_HEADS, RWKV_HD)
    kh = split_heads(k, RWKV_HEADS, RWKV_HD)
    vh = split_heads(v.astype(f32), RWKV_HEADS, RWKV_HD)
    ah = split_heads(a, RWKV_HEADS, RWKV_HD)
    dh = split_heads(decay, RWKV_HEADS, RWKV_HD)
    y = rwkv7_recurrence(rh, dh, kh, vh, kk, ah)
    mean = jnp.mean(y, axis=-1, keepdims=True)
    var = jnp.mean(jnp.square(y - mean), axis=-1, keepdims=True)
    y = (y - mean) * lax.rsqrt(var + LNX_EPS)
    y = y.reshape(y.shape[:2] + (RWKV_W,)) * lnx_g.astype(f32) + lnx_b.astype(f32)
    bonus = jnp.sum(rh * kh * r_k.astype(f32), axis=-1, keepdims=True) * vh
    y = (y + bonus.reshape(y.shape)) * g.astype(f32)
    return y.astype(dt), v_first


def stick_breaking_attention(q, k, v):
    t_len = q.shape[2]
    scale = SB_HD ** -0.5
    outs = []
    for blk in range(t_len // SB_BLOCK):
        start = blk * SB_BLOCK
        end = start + SB_BLOCK
        qb = q[:, :, start:end]
        kb = k[:, :, :end]
        vb = v[:, :, :end]
        z = jnp.einsum('bhqd,bhkd->bhqk', qb, kb).astype(jnp.float32) * scale
        t_idx = start + jnp.arange(SB_BLOCK)
        s_idx = jnp.arange(end)
        mask = s_idx[None, :] < t_idx[:, None]
        log_1mb = jnp.where(mask, jax.nn.log_sigmoid(-z), 0.0)
        log_between = lax.cumsum(log_1mb, axis=3, reverse=True) - log_1mb
        attn = jnp.where(mask, jnp.exp(jax.nn.log_sigmoid(z) + log_between), 0.0)
        outs.append(jnp.einsum('bhqk,bhkd->bhqd', attn.astype(v.dtype), vb))
    return jnp.concatenate(outs, axis=2)


def stick_breaking_branch(p, q_g, k_g):
    b, t, _ = p.shape
    q, k, v = jnp.split(p, 3, axis=-1)
    q = rms_norm(split_heads(q, SB_HEADS, SB_HD), q_g).transpose(0, 2, 1, 3)
    k = rms_norm(split_heads(k, SB_HEADS, SB_HD), k_g).transpose(0, 2, 1, 3)
    v = split_heads(v, SB_HEADS, SB_HD).transpose(0, 2, 1, 3)
    o = stick_breaking_attention(q, k, v)
    return o.transpose(0, 2, 1, 3).reshape(b, t, SB_W)


def memory_branch(q, mem, mem_g, w_kv, q_g, k_g):
    b, t, _ = q.shape
    kv = rms_norm(mem, mem_g) @ w_kv
    mk, mv = jnp.split(kv, 2, axis=-1)
    qh = rms_norm(split_heads(q, MEM_HEADS, MEM_HD), q_g)
    kh = rms_norm(split_heads(mk, MEM_HEADS, MEM_HD), k_g)
    vh = split_heads(mv, MEM_HEADS, MEM_HD)
    s = jnp.einsum('bthd,bmhd->bhtm', qh, kh).astype(jnp.float32) * (MEM_HD ** -0.5)
    pr = jax.nn.softmax(s, axis=-1).astype(vh.dtype)
    o = jnp.einsum('bhtm,bmhd->bthd', pr, vh)
    return o.reshape(b, t, MEM_W)


def conv_ffn(x, g, w_up, conv_w, conv_b, w_down):
    h = rms_norm(x, g) @ w_up
    t = h.shape[1]
    hp = jnp.pad(h, ((0, 0), (CONV_W - 1, 0), (0, 0)))
    acc = conv_b + hp[:, 0:t] * conv_w[0]
    for i in range(1, CONV_W):
        acc = acc + hp[:, i:i + t] * conv_w[i]
    gate, val = jnp.split(acc, 2, axis=-1)
    return (jax.nn.silu(gate) * val) @ w_down


def setup_inputs(seed: int = 0) -> dict:
    key = jax.random.key(seed)
    ks = jax.random.split(key, 40)
    f32 = jnp.float32
    nrm = lambda k, s, sc: jax.random.normal(k, s, f32) * sc
    gain = lambda k, s: 1.0 + 0.05 * jax.random.normal(k, s, f32)
    L = DEPTH
    return {
        "x": nrm(ks[0], (BATCH, SEQ, D_MODEL), 1.0),
        "mem": nrm(ks[1], (BATCH, MEM_LEN, D_MODEL), 1.0),
        "norm1_g": gain(ks[2], (L, D_MODEL)),
        "w_in": nrm(ks[3], (L, D_MODEL, IN_COLS), D_MODEL ** -0.5),
        "shift_mu": jax.random.uniform(ks[4], (L, RWKV_COLS), f32, 0.0, 1.0),
        "decay_w0": jax.random.uniform(ks[5], (L, RWKV_W), f32, -6.0, 1.0),
        "decay_w2": nrm(ks[6], (L, DECAY_LORA, RWKV_W), 0.5 * DECAY_LORA ** -0.5),
        "iclr_a0": nrm(ks[7], (L, RWKV_W), 0.1),
        "iclr_a2": nrm(ks[8], (L, ICLR_LORA, RWKV_W), ICLR_LORA ** -0.5),
        "gate_g2": nrm(ks[9], (L, GATE_LORA, RWKV_W), GATE_LORA ** -0.5),
        "k_k": 0.85 + 0.05 * jax.random.normal(ks[10], (L, RWKV_W), f32),
        "k_a": gain(ks[11], (L, RWKV_W)),
        "r_k": nrm(ks[12], (L, RWKV_HEADS, RWKV_HD), 0.1),
        "lnx_g": gain(ks[13], (L, RWKV_W)),
        "lnx_b": nrm(ks[14], (L, RWKV_W), 0.02),
        "vres_v0": 1.0 + 0.1 * jax.random.normal(ks[15], (L - 1, RWKV_W), f32),
        "vres_v1": nrm(ks[16], (L - 1, RWKV_W, VRES_LORA), RWKV_W ** -0.5),
        "vres_v2": nrm(ks[17], (L - 1, VRES_LORA, RWKV_W), VRES_LORA ** -0.5),
        "sb_q_norm_g": gain(ks[18], (L, SB_HD)),
        "sb_k_norm_g": gain(ks[19], (L, SB_HD)),
        "mem_norm_g": gain(ks[20], (L, D_MODEL)),
        "w_mem_kv": nrm(ks[21], (L, D_MODEL, 2 * MEM_W), D_MODEL ** -0.5),
        "mem_q_norm_g": gain(ks[22], (L, MEM_HD)),
        "mem_k_norm_g": gain(ks[23], (L, MEM_HD)),
        "w_branch": nrm(ks[24], (L, N_BRANCH, BRANCH_W, D_MODEL), BRANCH_W ** -0.5),
        "w_out": nrm(ks[25], (L, D_MODEL, D_MODEL), D_MODEL ** -0.5),
        "norm2_g": gain(ks[26], (L, D_MODEL)),
        "w_up": nrm(ks[27], (L, D_MODEL, 2 * D_FF), D_MODEL ** -0.5),
        "conv_w": nrm(ks[28], (L, CONV_W, 2 * D_FF), CONV_W ** -0.5),
        "conv_b": nrm(ks[29], (L, 2 * D_FF), 0.02),
        "w_down": nrm(ks[30], (L, D_FF, D_MODEL), D_FF ** -0.5),
    }


def reference(x, mem, norm1_g, w_in, shift_mu, decay_w0, decay_w2, iclr_a0, iclr_a2, gate_g2,
              k_k, k_a, r_k, lnx_g, lnx_b, vres_v0, vres_v1, vres_v2, sb_q_norm_g, sb_k_norm_g,
              mem_norm_g, w_mem_kv, mem_q_norm_g, mem_k_norm_g, w_branch, w_out, norm2_g,
              w_up, conv_w, conv_b, w_down):
    b, t, _ = x.shape
    cut = [RWKV_COLS, RWKV_COLS + 3 * SB_W, RWKV_COLS + 3 * SB_W + MEM_W]
    v_first = None
    for l in range(DEPTH):
        h = rms_norm(x, norm1_g[l])
        proj = h @ w_in[l]
        p_rwkv, p_sb, q_mem, gate_logits = jnp.split(proj, cut, axis=-1)
        vres = None if l == 0 else (vres_v0[l - 1], vres_v1[l - 1], vres_v2[l - 1])
        y_a, v_first = rwkv7_branch(p_rwkv, shift_mu[l], decay_w0[l], decay_w2[l], iclr_a0[l],
                                    iclr_a2[l], gate_g2[l], k_k[l], k_a[l], r_k[l], lnx_g[l],
                                    lnx_b[l], v_first, vres)
        y_b = stick_breaking_branch(p_sb, sb_q_norm_g[l], sb_k_norm_g[l])
        y_m = memory_branch(q_mem, mem, mem_norm_g[l], w_mem_kv[l], mem_q_norm_g[l], mem_k_norm_g[l])
        ys = jnp.stack([y_a, y_b.astype(y_a.dtype), y_m.astype(y_a.dtype)], axis=2)
        branch_out = jnp.einsum('btnc,ncd->btnd', ys, w_branch[l])
        gates = jax.nn.sigmoid(gate_logits.reshape(b, t, N_BRANCH, D_MODEL))
        merged = jnp.sum(gates * branch_out, axis=2)
        x = x + merged @ w_out[l]
        x = x + conv_ffn(x, norm2_g[l], w_up[l], conv_w[l], conv_b[l], w_down[l])
    return x
```

```python
import math
import os
import contextlib
import numpy as np
import concourse.bass as bass
import concourse.mybir as mybir
from concourse.bass_utils import run_bass_kernel_spmd

F32 = mybir.dt.float32
F32R = mybir.dt.float32r
BF16 = mybir.dt.bfloat16
AF = mybir.ActivationFunctionType
ALU = mybir.AluOpType

D = 1024
T = 2048
TT = 512
NT = T // TT
RT = 128
C = 64
DFF = 2816
INC = 6912
PVS = 320
NCONST = 1728
SMW = 2304
DEPTH = 2
C0 = math.exp(-0.5)

DEBUG = {}


class Res:
    __slots__ = ("name", "last_w", "readers")

    def __init__(self, name):
        self.name = name
        self.last_w = None
        self.readers = {}


class V:
    __slots__ = ("t", "ap")

    def __init__(self, t, ap):
        self.t = t
        self.ap = ap

    def __getitem__(self, k):
        return V(self.t, self.ap[k])

    def bc(self, shape):
        return V(self.t, self.ap.to_broadcast(list(shape)))

    def cast(self, dt):
        return V(self.t, self.ap.bitcast(dt))

    def re(self, pat, **kw):
        return V(self.t, self.ap.rearrange(pat, **kw))


class Tl(Res):
    def __init__(self, h, name, ap=None):
        Res.__init__(self, name)
        self.h = h
        self._ap = ap
        self.dma_sem = None
        self.dma_cnt = 0

    def __getitem__(self, k):
        base = self._ap if self._ap is not None else self.h
        return V(self, base[k])

    @property
    def v(self):
        return self[:]


ENGS = ["pe", "dve", "act", "pool", "sp"]


class Prog:
    def __init__(self, nc):
        self.nc = nc
        self.q = {e: [] for e in ENGS}
        self.sem = {e: nc.alloc_semaphore(f"prog_{e}") for e in ENGS}
        self.cnt = {e: 0 for e in ENGS}
        self.known = {e: {} for e in ENGS}
        self.n = 0
        self.out_tokens = []
        self.dma_tokens = {}
        self.marks = []
        self.pending_pool = None
        self.ghost = {}
        self.sem_pool = []
        self.dma_tiles = []

    def sb(self, stack, shape, dtype=F32, name="t"):
        self.n += 1
        h = stack.enter_context(self.nc.sbuf_tensor(f"{name}_{self.n}", list(shape), dtype))
        t = Tl(h, f"{name}_{self.n}")
        t.persist = getattr(stack, "is_root", False)
        t.readers = dict(self.ghost)
        return t

    def ps(self, stack, shape, dtype=F32, name="p"):
        self.n += 1
        h = stack.enter_context(self.nc.psum_tensor(f"{name}_{self.n}", list(shape), dtype))
        return Tl(h, f"{name}_{self.n}")

    def _collect(self, eng, reads, writes, sync_same):
        deps = {}

        def add(tok):
            if tok is None:
                return
            s, v = tok
            if deps.get(s, 0) < v:
                deps[s] = v

        for t in reads:
            add(t.last_w)
        for t in writes:
            add(t.last_w)
            for s, v in t.readers.items():
                add((s, v))
        waits = []
        kn = self.known[eng]
        own = self.sem[eng]
        for s, v in deps.items():
            if s is own and not sync_same:
                continue
            if kn.get(s, 0) >= v:
                continue
            kn[s] = v
            waits.append((s, v))
        return waits

    def _commit(self, tok, reads, writes):
        s, v = tok
        for t in reads:
            if t.readers.get(s, 0) < v:
                t.readers[s] = v
        for t in writes:
            t.last_w = tok
            t.readers = {}

    def op(self, eng, fn, reads=(), writes=(), sync_same=True):
        if eng == "pool":
            self._flush_pool(list(reads) + list(writes))
        waits = self._collect(eng, reads, writes, sync_same)
        self.cnt[eng] += 1
        tok = (self.sem[eng], self.cnt[eng])
        self.q[eng].append((waits, fn, tok, 1))
        self._commit(tok, reads, writes)
        return tok

    def dma(self, q, out, in_, semtile=None, is_output=False):
        reads = [in_.t]
        writes = [out.t]
        if q == "pool":
            self._flush_pool(reads + writes)
        waits = self._collect(q, reads, writes, True)
        if semtile is None:
            semtile = out.t if isinstance(out.t, Tl) and out.t.h is not None and not getattr(out.t, "is_dram", False) else in_.t
        if semtile.dma_sem is None:
            if self.sem_pool:
                semtile.dma_sem, semtile.dma_cnt = self.sem_pool.pop()
            else:
                semtile.dma_sem = self.nc.alloc_semaphore(f"dma_{semtile.name}")
            self.dma_tiles.append(semtile)
        semtile.dma_cnt += 16
        tok = (semtile.dma_sem, semtile.dma_cnt)
        oa, ia = out.ap, in_.ap
        self.q[q].append((waits, lambda e: e.dma_start(out=oa, in_=ia), tok, 16))
        self._commit(tok, reads, writes)
        self.dma_tokens[semtile.dma_sem] = tok
        if is_output:
            self.out_tokens.append(tok)
        return tok

    def _flush_pool(self, res):
        if self.pending_pool is None or all(getattr(t, "persist", False) for t in res):
            return
        waits = []
        kn = self.known["pool"]
        for s, v in self.pending_pool:
            if kn.get(s, 0) < v:
                kn[s] = v
                waits.append((s, v))
        if waits:
            self.q["pool"].append((waits, None, None, 0))
        self.pending_pool = None
        self.ghost = {}

    def mark(self, name):
        self.marks.append((name, dict(self.cnt)))

    def barrier(self):
        toks = [(self.sem[e], self.cnt[e]) for e in ENGS if self.cnt[e] > 0]
        toks += list(self.dma_tokens.values())
        g = {}
        for s_, v in toks:
            if g.get(s_, 0) < v:
                g[s_] = v
        self.ghost = g
        keep = []
        for t in self.dma_tiles:
            if getattr(t, "persist", False):
                keep.append(t)
            else:
                self.sem_pool.append((t.dma_sem, t.dma_cnt))
                t.dma_sem = None
        self.dma_tiles = keep

    def mm(self, out, lhsT, rhs, start=True, stop=True):
        oa, la, ra = out.ap, lhsT.ap, rhs.ap
        self.op("pe", lambda e: e.matmul(oa, lhsT=la, rhs=ra, start=start, stop=stop),
                reads=[lhsT.t, rhs.t], writes=[out.t], sync_same=False)

    def tr(self, out, in_, ident):
        oa, ia, da = out.ap, in_.ap, ident.ap
        self.op("pe", lambda e: e.transpose(oa, ia, da), reads=[in_.t, ident.t], writes=[out.t], sync_same=False)

    def act(self, out, in_, func, bias=None, scale=None):
        oa, ia = out.ap, in_.ap
        reads = [in_.t]
        kw = {}
        if bias is not None:
            if isinstance(bias, V):
                kw["bias"] = bias.ap
                reads.append(bias.t)
            else:
                kw["bias"] = bias
        if scale is not None:
            if isinstance(scale, V):
                kw["scale"] = scale.ap
                reads.append(scale.t)
            else:
                kw["scale"] = scale
        self.op("act", lambda e: e.activation(out=oa, in_=ia, func=func, **kw), reads=reads, writes=[out.t])

    def tt(self, out, in0, in1, op, eng="dve"):
        oa, a, b = out.ap, in0.ap, in1.ap
        self.op(eng, lambda e: e.tensor_tensor(out=oa, in0=a, in1=b, op=op), reads=[in0.t, in1.t], writes=[out.t])

    def ts(self, out, in0, s1, op0, s2=None, op1=None, eng="dve"):
        oa, a = out.ap, in0.ap
        reads = [in0.t]
        if isinstance(s1, V):
            reads.append(s1.t)
            s1 = s1.ap
        if isinstance(s2, V):
            reads.append(s2.t)
            s2 = s2.ap
        if op1 is None:
            self.op(eng, lambda e: e.tensor_scalar(out=oa, in0=a, scalar1=s1, scalar2=None, op0=op0), reads=reads, writes=[out.t])
        else:
            self.op(eng, lambda e: e.tensor_scalar(out=oa, in0=a, scalar1=s1, scalar2=s2, op0=op0, op1=op1), reads=reads, writes=[out.t])

    def stt(self, out, in0, scalar, in1, op0, op1):
        oa, a, b = out.ap, in0.ap, in1.ap
        reads = [in0.t, in1.t]
        if isinstance(scalar, V):
            reads.append(scalar.t)
            scalar = scalar.ap
        self.op("dve", lambda e: e.scalar_tensor_tensor(out=oa, in0=a, scalar=scalar, in1=b, op0=op0, op1=op1),
                reads=reads, writes=[out.t])

    def cp(self, out, in_, eng="dve"):
        oa, ia = out.ap, in_.ap
        if eng == "act":
            self.op("act", lambda e: e.copy(out=oa, in_=ia), reads=[in_.t], writes=[out.t])
        else:
            self.op(eng, lambda e: e.tensor_copy(out=oa, in_=ia), reads=[in_.t], writes=[out.t])

    def recip(self, out, in_):
        oa, ia = out.ap, in_.ap
        self.op("dve", lambda e: e.reciprocal(out=oa, in_=ia), reads=[in_.t], writes=[out.t])

    def scan(self, out, d0, d1, init, op0, op1):
        oa, a, b = out.ap, d0.ap, d1.ap
        self.op("dve", lambda e: e.tensor_tensor_scan(out=oa, data0=a, data1=b, initial=init, op0=op0, op1=op1),
                reads=[d0.t, d1.t], writes=[out.t])

    def memset(self, out, val, eng="dve"):
        oa = out.ap
        self.op(eng, lambda e: e.memset(oa, val), reads=[], writes=[out.t])

    def emit(self):
        nc = self.nc
        final = {}
        for s, v in self.out_tokens:
            if final.get(s, 0) < v:
                final[s] = v
        final_waits = list(final.items())
        q = self.q

        def run(e, lst, extra=()):
            for waits, fn, tok, inc in lst:
                for s, v in waits:
                    e.wait_ge(s, v)
                if fn is not None:
                    fn(e).then_inc(tok[0], inc)
            for s, v in extra:
                e.wait_ge(s, v)

        with nc.Block() as block:
            @block.tensor
            def _(e):
                run(e, q["pe"])

            @block.vector
            def _(e):
                run(e, q["dve"])

            @block.scalar
            def _(e):
                run(e, q["act"])

            @block.gpsimd
            def _(e):
                run(e, q["pool"])

            @block.sync
            def _(e):
                run(e, q["sp"], final_waits)


def build(n_layers=DEPTH, n_tiles=NT, dbg=None):
    nc = bass.Bass("TRN2", target_bir_lowering=False)
    P = Prog(nc)

    def dram(name, shape, kind, dt=F32):
        h = nc.dram_tensor(name, list(shape), dt, kind=kind)
        t = Tl(None, name, ap=h.ap())
        t.is_dram = True
        t.persist = True
        return t

    xT_d = dram("xT", [D, T], "ExternalInput")
    memT_d = dram("memT", [D, 256], "ExternalInput")
    pvec_d = dram("pvec", [128, DEPTH * PVS], "ExternalInput")
    smat_d = dram("smat", [DEPTH, 128, SMW], "ExternalInput")
    const_d = dram("consts", [128, NCONST], "ExternalInput")
    w_in_d = dram("w_in", [DEPTH, D, INC], "ExternalInput")
    w_kv_d = dram("w_mem_kv", [DEPTH, D, 1024], "ExternalInput")
    w_br_d = dram("w_branch", [DEPTH, 3, 512, D], "ExternalInput")
    w_out_d = dram("w_out", [DEPTH, D, D], "ExternalInput")
    w_up_d = dram("w_up", [DEPTH, D, 2 * DFF], "ExternalInput")
    w_dn_d = dram("w_down", [DEPTH, DFF, D], "ExternalInput")
    yT_d = dram("yT", [D, T], "ExternalOutput")
    xs_d = dram("xs", [D, T], "Internal")
    vf_d = dram("vf", [8, 64, T], "Internal")
    dbg_d = {}
    if dbg:
        for k, shp in dbg.items():
            dbg_d[k] = dram("dbg_" + k, shp, "ExternalOutput")

    root = contextlib.ExitStack()
    root.is_root = True
    CT = P.sb(root, [128, NCONST], F32, "CT")
    CB = P.sb(root, [128, 1024], BF16, "CB")
    RM8 = P.sb(root, [64, 8, RT], BF16, "RM8")
    PV = P.sb(root, [128, DEPTH * PVS], F32, "PV")
    OMM = P.sb(root, [128, DEPTH * 32], F32, "OMM")
    SM = P.sb(root, [128, SMW], F32, "SM")
    SMb = P.sb(root, [128, 1536], BF16, "SMb")
    KT = P.sb(root, [128, 4, T], BF16, "KT")
    VC = P.sb(root, [128, T // 128, 512], BF16, "VC")
    MK = P.sb(root, [128, 4, 256], BF16, "MK")
    MV = P.sb(root, [128, 2, 512], BF16, "MV")
    XT = P.sb(root, [128, 8, TT], F32, "XT")
    HT = P.sb(root, [128, 8, TT], BF16, "HT")
    NWB = 6
    WT = [P.sb(root, [128, 8, 256], BF16, f"WT{i}") for i in range(NWB)]
    Hst = P.sb(root, [64, 8, 64], F32, "Hst")
    Hb = P.sb(root, [64, 8, 64], BF16, "Hb")
    PREV = P.sb(root, [128, 32], F32, "PREV")
    HALO = P.sb(root, [128, 44, 2], F32, "HALO")
    YA = P.sb(root, [64, 8, TT], BF16, "YA")
    YB = P.sb(root, [64, 8, TT], BF16, "YB")
    YM = P.sb(root, [128, 4, TT], BF16, "YM")
    PS = [P.ps(root, [128, 512], F32, f"PS{i}") for i in range(8)]
    st = {"ps": 0, "w": 0, "wd": 0}

    def nps(k=6):
        i = st["ps"] % k
        st["ps"] += 1
        return PS[i]

    ident = CT[:, 0:128]
    ones = CT[:, 128:256]
    bones = CT[:, 256:384]
    su = CT[0:64, 384:448]
    iu = CT[0:64, 448:512]
    sl = CT[0:64, 512:576]
    id64 = CT[0:64, 576:640]
    smk = CT[:, 640:768]
    rmask = CT[0:64, 768:896]
    tri = CT[:, 896:1024]
    EPS6 = CT[:, 1024:1025]
    EPS64 = CT[:, 1025:1026]
    EPS128 = CT[:, 1026:1027]
    EPSLN = CT[:, 1027:1028]
    ONE = CT[:, 1028:1029]
    EPS18 = CT[:, 1029:1030]
    tri_b = CB[:, 0:128]
    ones_b = CB[:, 128:256]
    Cm = CT[0:64, 1664:1728]
    ident_b = CB[:, 384:512]
    bones_b = CB[:, 896:1024]
    rmask8 = RM8.v.re("p h t -> p (h t)")
    ntri_b = CB[:, 512:640]
    nones_b = CB[:, 640:768]
    negm_b = CB[:, 768:896]
    sel0_b = CB[:, 256:384]

    def m8(mv):
        return V(mv.t, mv.ap.rearrange("p (o c) -> p o c", o=1).to_broadcast([64, 8, 64]))

    P.dma("sp", CT.v, const_d.v)
    P.dma("sp", PV.v, pvec_d.v)
    P.cp(CB[:, 0:128], tri)
    P.cp(CB[:, 128:256], ones)
    P.cp(CB[:, 256:384], CT[:, 1152:1280])
    P.cp(CB[:, 384:512], ident)
    P.cp(CB[:, 512:896], CT[:, 1280:1664])
    P.cp(CB[:, 896:1024], bones)
    P.cp(RM8.v, rmask.re("p (o t) -> p o t", o=1).bc([64, 8, RT]))
    for l in range(DEPTH):
        P.ts(OMM[:, l * 32:l * 32 + 27], PV[:, l * PVS + 24:l * PVS + 51], -1.0, ALU.mult, 1.0, ALU.add)

    def wload(src_rows_view, kc, ncols):
        wt = WT[st["w"] % NWB]
        st["w"] += 1
        half = max(1, kc // 2)
        for a in range(0, kc, half):
            b = min(kc, a + half)
            P.dma("pool", wt[:, a:b, 0:ncols], src_rows_view[:, a:b, :])
        return wt

    def w1024(wd, l, c0, ncols):
        return wd[l].re("(c p) n -> p c n", p=128)[:, :, c0:c0 + ncols]

    def rmsnorm(phase, src, n, gcol0, dst, N):
        SQ = [P.sb(phase, [128, N], BF16, "SQ") for _ in range(4)]
        RS = P.sb(phase, [128, N], F32, "RS")
        ss = nps()
        for c in range(n):
            s = SQ[c % 4]
            P.act(s.v, src[:, c, :], AF.Square)
            P.mm(ss[:, 0:N], ones_b, s.v, start=(c == 0), stop=(c == n - 1))
        P.act(RS.v, ss[:, 0:N], AF.Ln, bias=EPS6, scale=1.0 / (128 * n))
        P.act(RS.v, RS.v, AF.Exp, scale=-0.5)
        for c in range(n):
            P.stt(dst[:, c, :], src[:, c, :], PV[:, gcol0 + c:gcol0 + c + 1], RS.v, ALU.mult, ALU.mult)

    def dump(name, view):
        if dbg and name in dbg_d:
            P.dma("pool", dbg_d[name].v, view)

    for l in range(n_layers):
        pv0 = l * PVS
        last = (l == n_layers - 1)
        xsrc = xT_d if l == 0 else xs_d
        xdst = yT_d if last else xs_d
        P.dma("sp", SM.v, smat_d[l])
        P.cp(SMb.v, SM[:, 0:1536])
        P.memset(Hst.v, 0.0)
        P.memset(Hb.v, 0.0)
        P.memset(PREV.v, 0.0)
        P.memset(HALO.v, 0.0)

        with contextlib.ExitStack() as ph:
            MT = P.sb(ph, [128, 8, 256], F32, "MT")
            MN = P.sb(ph, [128, 8, 256], BF16, "MN")
            KR = P.sb(ph, [128, 256], F32, "KR")
            SQ = P.sb(ph, [128, 256], F32, "SQm")
            RS = P.sb(ph, [128, 256], F32, "RSm")
            P.dma("sp", MT.v, memT_d.v.re("(c p) n -> p c n", p=128))
            rmsnorm(ph, MT, 8, pv0 + 16, MN, 256)
            for hp in range(2):
                wt = wload(w1024(w_kv_d, l, hp * 256, 256), 8, 256)
                for hh in range(2):
                    h = hp * 2 + hh
                    pk = nps()
                    for kc in range(8):
                        P.mm(pk[:, 0:256], wt[:, kc, hh * 128:(hh + 1) * 128], MN[:, kc, :], start=(kc == 0), stop=(kc == 7))
                    P.cp(KR.v, pk[:, 0:256], eng="act")
                    P.act(SQ.v, KR.v, AF.Square)
                    ss = nps()
                    P.mm(ss[:, 0:256], ones, SQ.v)
                    P.act(RS.v, ss[:, 0:256], AF.Ln, bias=EPS6, scale=1.0 / 128)
                    P.act(RS.v, RS.v, AF.Exp, scale=-0.5)
                    P.stt(MK[:, h, :], KR.v, PV[:, pv0 + 118:pv0 + 119], RS.v, ALU.mult, ALU.mult)
            for vp in range(2):
                wt = wload(w1024(w_kv_d, l, 512 + vp * 256, 256), 8, 256)
                for mb in range(2):
                    pvv = nps()
                    for kc in range(8):
                        P.mm(pvv[:, 0:256], MN[:, kc, mb * 128:(mb + 1) * 128], wt[:, kc, :], start=(kc == 0), stop=(kc == 7))
                    P.cp(MV[:, mb, vp * 256:(vp + 1) * 256], pvv[:, 0:256], eng="act")
            P.barrier()

        for ti in range(n_tiles):
            t0 = ti * TT
            with contextlib.ExitStack() as ph:
                for a in range(0, 8, 4):
                    P.dma("sp", XT[:, a:a + 4, :], xsrc.v.re("(c p) n -> p c n", p=128)[:, a:a + 4, t0:t0 + TT])
                rmsnorm(ph, XT, 8, pv0 + 0, HT, TT)
                P.barrier()
            if l == 0 and ti == 0:
                dump("h", HT.v)

            for sub in range(TT // RT):
                q0 = sub * RT
                g0 = t0 + q0
                with contextlib.ExitStack() as ph:
                    GG = P.sb(ph, [64, 8, RT], F32, "GG")
                    EV = P.sb(ph, [64, 8, RT], F32, "EV")
                    GC = P.sb(ph, [64, 8, 2], F32, "GC")
                    RTb, ATb, BTb, KTb, VVb = [P.sb(ph, [64, 8, RT], BF16, n) for n in ("RTb", "ATb", "BTb", "KTb", "VVb")]
                    VTK = P.sb(ph, [64, 2, 512], BF16, "VTK")
                    BTK = P.sb(ph, [64, 2, 512], BF16, "BTK")
                    KTK = P.sb(ph, [64, 2, 512], BF16, "KTK")
                    ph2 = contextlib.ExitStack()

                    def fb(nm):
                        return P.sb(ph2, [64, 8, RT], F32, nm)
                    E1, E2, E3, AA, SG, CU, RR, VV = [fb(n) for n in ("E1", "E2", "E3", "AA", "SG", "CU", "RR", "VV")]
                    PR = [P.sb(ph2, [64, 4, RT + 1], F32, "PR")] * 2
                    TM = [P.sb(ph2, [64, 4, RT], F32, "TM")] * 2
                    WL = P.sb(ph2, [64, RT], BF16, "WL")
                    AL = P.sb(ph2, [64, RT], BF16, "AL")
                    GL = P.sb(ph2, [128, RT + 1], F32, "GL")
                    GS = P.sb(ph2, [128, RT], F32, "GS")
                    GSb = P.sb(ph2, [128, RT], BF16, "GSb")
                    prc = [0]

                    def shift4(psv, nh, prev0, mu0, dst):
                        pom = PR[prc[0] % 2]
                        pmu = TM[prc[0] % 2]
                        prc[0] += 1
                        prevv = PREV[0:64, prev0:prev0 + nh].re("p (h o) -> p h o", o=1)
                        muv = PV[0:64, pv0 + mu0:pv0 + mu0 + nh].re("p (h o) -> p h o", o=1)
                        for hh in range(nh):
                            mu1 = PV[0:64, pv0 + mu0 + hh:pv0 + mu0 + hh + 1]
                            om1 = OMM[0:64, l * 32 + (mu0 - 24) + hh:l * 32 + (mu0 - 24) + hh + 1]
                            P.act(pom[:, hh, 0:RT], psv[:, hh, :], AF.Identity, scale=om1)
                            P.act(pmu[:, hh, 1:RT], psv[:, hh, 0:RT - 1], AF.Identity, scale=mu1)
                        P.tt(pmu[:, 0:nh, 0:1], prevv, muv, ALU.mult)
                        P.cp(prevv, psv[:, :, RT - 1:RT], eng="act")
                        P.tt(dst, pom[:, 0:nh, 0:RT], pmu[:, 0:nh, :], ALU.add)

                    hsl = HT[:, :, q0:q0 + RT]
                    wt = wload(w1024(w_in_d, l, 1536, 256), 8, 256)
                    pa = nps()
                    for j in range(2):
                        for kc in range(8):
                            P.mm(pa[0:64, j * RT:(j + 1) * RT], wt[:, kc, j * 64:(j + 1) * 64], hsl[:, kc, :], start=(kc == 0), stop=(kc == 7))
                    WA = P.sb(ph2, [64, 2, RT], F32, "WA")
                    shift4(pa[0:64, 0:2 * RT].re("p (h t) -> p h t", h=2), 2, 24, 48, WA.v)
                    P.act(WL.v, WA[:, 0, :], AF.Tanh)
                    P.cp(AL.v, WA[:, 1, :])
                    pg = nps()
                    for kc in range(8):
                        P.mm(pg[:, 0:RT], wt[:, kc, 128:256], hsl[:, kc, :], start=(kc == 0), stop=(kc == 7))
                    P.cp(GL[:, 0:1], PREV[:, 26:27])
                    P.cp(GL[:, 1:RT + 1], pg[:, 0:RT], eng="act")
                    P.cp(PREV[:, 26:27], GL[:, RT:RT + 1])
                    P.ts(GS.v, GL[:, 0:RT], PV[:, pv0 + 50:pv0 + 51], ALU.mult)
                    P.stt(GS.v, GL[:, 1:RT + 1], OMM[:, l * 32 + 26:l * 32 + 27], GS.v, ALU.mult, ALU.add)
                    P.act(GSb.v, GS.v, AF.Sigmoid)
                    for hq in range(2):
                        p1 = nps()
                        p2 = nps()
                        p3 = nps()
                        for hh in range(4):
                            h = hq * 4 + hh
                            P.mm(p1[0:64, hh * RT:(hh + 1) * RT], SMb[0:64, h * 64:(h + 1) * 64], WL.v)
                            P.mm(p2[0:64, hh * RT:(hh + 1) * RT], SMb[0:64, 512 + h * 64:512 + (h + 1) * 64], AL.v)
                            P.mm(p3[0:64, hh * RT:(hh + 1) * RT], SMb[:, 1024 + h * 64:1024 + (h + 1) * 64], GSb.v)
                        for hh in range(4):
                            h = hq * 4 + hh
                            P.act(SG[:, h, :], p1[0:64, hh * RT:(hh + 1) * RT], AF.Sigmoid, bias=PV[0:64, pv0 + 51 + h:pv0 + 52 + h])
                            P.act(AA[:, h, :], p2[0:64, hh * RT:(hh + 1) * RT], AF.Sigmoid, bias=PV[0:64, pv0 + 59 + h:pv0 + 60 + h])
                        P.cp(GG[:, hq * 4:hq * 4 + 4, :], p3[0:64, 0:4 * RT].re("p (h t) -> p h t", h=4))
                    P.scan(CU.v.re("p h t -> p (h t)"), rmask8, SG.v.re("p h t -> p (h t)"), 0.0, ALU.mult, ALU.add)
                    P.act(E1.v, CU.v, AF.Exp, scale=-C0)
                    P.act(E2.v, CU.v, AF.Exp, scale=C0)
                    P.tt(SG.v, CU.v, SG.v, ALU.subtract)
                    P.act(E3.v, SG.v, AF.Exp, scale=-C0)
                    P.cp(GC.v, E1.v.re("p h (c t) -> p h c t", c=2)[:, :, :, C - 1])

                    P.mark('rw_lora')
                    def proj_heads(col0, prev0, mu0, dst):
                        for hq in range(2):
                            wt = wload(w1024(w_in_d, l, col0 + hq * 256, 256), 8, 256)
                            pb = nps()
                            for hh in range(4):
                                for kc in range(8):
                                    P.mm(pb[0:64, hh * RT:(hh + 1) * RT], wt[:, kc, hh * 64:(hh + 1) * 64], hsl[:, kc, :], start=(kc == 0), stop=(kc == 7))
                            shift4(pb[0:64, 0:4 * RT].re("p (h t) -> p h t", h=4), 4, prev0 + hq * 4, mu0 + hq * 4, dst[:, hq * 4:hq * 4 + 4, :])

                    KK = CU
                    proj_heads(512, 8, 32, KK.v)

                    def pbc(col):
                        return PV[0:64, pv0 + col:pv0 + col + 8].re("p (h o) -> p h o", o=1).bc([64, 8, RT])

                    T1 = SG
                    T2 = P.sb(ph2, [64, 8, RT], F32, "T2")
                    P.tt(T1.v, KK.v, pbc(67), ALU.mult)
                    P.act(VVb.v, T1.v, AF.Square)
                    for hq in range(2):
                        pn = nps()
                        P.mm(pn[0:64, 0:4 * RT], ones_b[0:64, 0:64], VVb[:, hq * 4:hq * 4 + 4, :].re("p h t -> p (h t)"))
                        P.act(T2[:, hq * 4:hq * 4 + 4, :].re("p h t -> p (h t)"), pn[0:64, 0:4 * RT], AF.Ln, bias=EPS18[0:64, :])
                    P.act(T2.v, T2.v, AF.Exp, scale=-0.5)
                    P.tt(T1.v, T1.v, T2.v, ALU.mult)
                    P.stt(ATb.v, T1.v, -1.0, E3.v, ALU.mult, ALU.mult)
                    P.tt(T1.v, T1.v, AA.v, ALU.mult)
                    P.ts(T2.v, AA.v, -1.0, ALU.add)
                    P.tt(T2.v, T2.v, pbc(75), ALU.mult)
                    P.stt(T2.v, T2.v, 1.0, KK.v, ALU.add, ALU.mult)
                    P.tt(KTb.v, T2.v, E2.v, ALU.mult)
                    P.tt(BTb.v, T1.v, E2.v, ALU.mult)
                    proj_heads(1024, 16, 40, VV.v)
                    proj_heads(0, 0, 24, RR.v)
                    P.mark('rw_proj')
                    P.tt(AA.v, RR.v, T2.v, ALU.mult)
                    P.tt(RTb.v, AA.v, pbc(83), ALU.mult)
                    for hq in range(2):
                        pn = nps()
                        P.mm(pn[0:64, 0:4 * RT], ones_b[0:64, 0:64], RTb[:, hq * 4:hq * 4 + 4, :].re("p h t -> p (h t)"))
                        P.cp(EV[:, hq * 4:hq * 4 + 4, :].re("p h t -> p (h t)"), pn[0:64, 0:4 * RT], eng="act")
                    P.tt(RTb.v, RR.v, E1.v, ALU.mult)
                    KTt, BTt, ATt, RTt = KTb, BTb, ATb, RTb
                    vfv = vf_d.v.re("h p t -> p h t")[:, :, g0:g0 + RT]
                    if l == 0:
                        P.dma("sp", vfv, VV.v)
                    else:
                        VF = RR
                        P.dma("sp", VF.v, vfv)
                        p32 = nps()
                        for h in range(8):
                            P.mm(p32[0:32, 0:RT], SM[0:64, 1536 + h * 32:1536 + (h + 1) * 32], VV[:, h, :], start=(h == 0), stop=(h == 7))
                        T32 = P.sb(ph2, [32, RT], F32, "T32")
                        P.cp(T32.v, p32[0:32, 0:RT], eng="act")
                        for hq in range(2):
                            pz = nps()
                            for hh in range(4):
                                h = hq * 4 + hh
                                P.mm(pz[0:64, hh * RT:(hh + 1) * RT], SM[0:32, 1792 + h * 64:1792 + (h + 1) * 64], T32.v)
                            for hh in range(4):
                                h = hq * 4 + hh
                                P.act(T2[:, h, :], pz[0:64, hh * RT:(hh + 1) * RT], AF.Sigmoid, bias=PV[0:64, pv0 + 107 + h:pv0 + 108 + h])
                        P.tt(T1.v, VF.v, VV.v, ALU.subtract)
                        P.tt(T1.v, T1.v, T2.v, ALU.mult)
                        P.tt(VV.v, VV.v, T1.v, ALU.add)
                    P.tt(EV.v, EV.v, VV.v, ALU.mult)
                    P.mark('rw_prep')
                    P.cp(VVb.v, VV.v, eng="act")
                    for (src, dstk) in ((VVb, VTK), (BTt, BTK), (KTt, KTK)):
                        for c in range(2):
                            pt = nps(8)
                            ptb = pt.v.cast(BF16)
                            for h in range(8):
                                P.tr(ptb[0:64, h * 64:(h + 1) * 64], src[:, h, c * C:(c + 1) * C], ident_b[0:64, 0:64])
                            P.cp(dstk[:, c, :], ptb[0:64, 0:512], eng=("act" if c else "dve"))
                    P.barrier()
                    ph2.close()
                    P.mark('rw_tr')
                    def t64(nm, dt=BF16):
                        return P.sb(ph, [64, 8, 64], dt, nm)
                    CH = []
                    for c in range(2):
                        CH.append(dict(NM=t64("NM"), LM=t64("LM"), N2=t64("N2"), L2=t64("L2"), MAK=t64("MAK"), MRB=t64("MRB"),
                                       MRK=t64("MRK"), PP=t64("PP", F32), PPb=t64("PPb")))
                    WS, US = t64("WS"), t64("US")
                    YSc = [t64("YS", F32) for _ in range(2)]
                    YQc = [t64("YQ", F32) for _ in range(2)]
                    MEc = [t64("MEAN", F32) for _ in range(2)]

                    def gram(dst, a, b, c, mask):
                        pgm = nps(8)
                        for h in range(8):
                            P.mm(pgm[0:64, h * 64:(h + 1) * 64], a[:, h, c * C:(c + 1) * C], b[:, h, c * C:(c + 1) * C])
                        P.tt(dst.v, pgm[0:64, :].re("p (h t) -> p h t", h=8), m8(mask), ALU.mult)

                    def mm8(dst_ps, a, b):
                        for h in range(8):
                            P.mm(dst_ps[0:64, h * 64:(h + 1) * 64], a[:, h, :], b[:, h, :])

                    def v8(ps):
                        return ps[0:64, :].re("p (h t) -> p h t", h=8)

                    for c in range(2):
                        d = CH[c]
                        gram(d["NM"], BTt, ATt, c, su)
                        gram(d["LM"], ATt, BTt, c, sl)
                        gram(d["MAK"], KTt, ATt, c, su)
                        gram(d["MRB"], BTt, RTt, c, iu)
                        gram(d["MRK"], KTt, RTt, c, iu)
                        P.tt(d["PP"].v, d["NM"].v, m8(id64), ALU.add)
                        P.cp(d["PPb"].v, d["PP"].v, eng="act")
                        d["cur"] = (d["LM"], d["NM"], d["L2"], d["N2"])
                    for lev in range(1, 6):
                        pls, pns, pqs = [], [], []
                        for c in range(2):
                            Lk, Nk, Lo, No = CH[c]["cur"]
                            pl = nps(8)
                            mm8(pl, Nk, Lk)
                            pls.append(pl)
                            if lev < 5:
                                pn = nps(8)
                                mm8(pn, Lk, Nk)
                                pns.append(pn)
                        for c in range(2):
                            Lk, Nk, Lo, No = CH[c]["cur"]
                            P.cp(Lo.v, v8(pls[c]), eng="act")
                            if lev < 5:
                                P.cp(No.v, v8(pns[c]))
                            CH[c]["cur"] = (Lo, No, Lk, Nk)
                        for c in range(2):
                            Lk = CH[c]["cur"][0]
                            pq = nps(8)
                            mm8(pq, Lk, CH[c]["PPb"])
                            pqs.append(pq)
                        for c in range(2):
                            P.tt(CH[c]["PP"].v, CH[c]["PP"].v, v8(pqs[c]), ALU.add)
                            P.cp(CH[c]["PPb"].v, CH[c]["PP"].v, eng="act")
                    P.mark('rw_inv')
                    for c in range(2):
                        d = CH[c]
                        MAK, MRB, MRK, PPb = d["MAK"], d["MRB"], d["MRK"], d["PPb"]
                        pw = nps(8)
                        for h in range(8):
                            P.mm(pw[0:64, h * 64:(h + 1) * 64], ATt[:, h, c * C:(c + 1) * C], Hb[:, h, :], start=True, stop=False)
                            P.mm(pw[0:64, h * 64:(h + 1) * 64], MAK[:, h, :], VTK[:, c, h * 64:(h + 1) * 64], start=False, stop=True)
                        P.cp(WS.v, v8(pw), eng="act")
                        pu = nps(8)
                        mm8(pu, PPb, WS)
                        P.cp(US.v, v8(pu), eng="act")
                        py = nps(8)
                        for h in range(8):
                            o = py[0:64, h * 64:(h + 1) * 64]
                            P.mm(o, Hb[:, h, :], RTt[:, h, c * C:(c + 1) * C], start=True, stop=False)
                            P.mm(o, US[:, h, :], MRB[:, h, :], start=False, stop=False)
                            P.mm(o, VTK[:, c, h * 64:(h + 1) * 64], MRK[:, h, :], start=False, stop=True)
                        phh = nps(8)
                        for h in range(8):
                            o = phh[0:64, h * 64:(h + 1) * 64]
                            P.mm(o, BTK[:, c, h * 64:(h + 1) * 64], US[:, h, :], start=True, stop=False)
                            P.mm(o, KTK[:, c, h * 64:(h + 1) * 64], VTK[:, c, h * 64:(h + 1) * 64], start=False, stop=True)
                        P.tt(Hst.v, Hst.v, v8(phh), ALU.add)
                        P.tt(Hst.v, Hst.v, GC[:, :, c:c + 1].bc([64, 8, 64]), ALU.mult)
                        P.cp(Hb.v, Hst.v, eng="act")
                        P.mark('rw_state')
                        P.cp(YSc[c].v, v8(py), eng="act")
                    fl = "p h t -> p (h t)"
                    pcs, pqs2 = [], []
                    for c in range(2):
                        pc = nps(8)
                        P.mm(pc[0:64, :], Cm, YSc[c].v.re(fl))
                        pcs.append(pc)
                    for c in range(2):
                        P.act(YQc[c].v.re(fl), pcs[c][0:64, :], AF.Square)
                    for c in range(2):
                        pq2 = nps(8)
                        P.mm(pq2[0:64, :], ones[0:64, 0:64], YQc[c].v.re(fl))
                        pqs2.append(pq2)
                    for c in range(2):
                        P.act(MEc[c].v.re(fl), pqs2[c][0:64, :], AF.Ln, bias=EPSLN[0:64, :], scale=1.0 / 64)
                    for c in range(2):
                        P.act(MEc[c].v, MEc[c].v, AF.Exp, scale=-0.5)
                    for c in range(2):
                        P.tt(YSc[c].v.re(fl), pcs[c][0:64, :], MEc[c].v.re(fl), ALU.mult)
                    lg = PV[0:64, pv0 + 91:pv0 + 99].re("p (h o) -> p h o", o=1).bc([64, 8, 64])
                    lb = PV[0:64, pv0 + 99:pv0 + 107].re("p (h o) -> p h o", o=1).bc([64, 8, 64])
                    for c in range(2):
                        P.tt(YSc[c].v, YSc[c].v, lg, ALU.mult)
                    for c in range(2):
                        P.tt(YSc[c].v, YSc[c].v, lb, ALU.add)
                    for c in range(2):
                        P.tt(YSc[c].v, YSc[c].v, EV[:, :, c * C:(c + 1) * C], ALU.add)
                    for c in range(2):
                        P.tt(YA[:, :, q0 + c * C:q0 + (c + 1) * C], YSc[c].v, GG[:, :, c * C:(c + 1) * C], ALU.mult)
                    if True:
                        P.mark('rw_epi')
                    P.barrier()
            if l == 0 and ti == 0:
                dump("ya", YA.v)

            with contextlib.ExitStack() as ph:
                QT = P.sb(ph, [128, 4, TT], BF16, "QT")
                QR = [P.sb(ph, [128, TT], F32, "QR") for _ in range(2)]
                SQ = [P.sb(ph, [128, TT], BF16, "SQs") for _ in range(2)]
                RS = [P.sb(ph, [128, TT], F32, "RSs") for _ in range(2)]
                for hp2 in range(2):
                    wqk = [wload(w1024(w_in_d, l, 1792 + which * 512 + hp2 * 256, 256), 8, 256) for which in range(2)]
                    for pp in range(2):
                        p = hp2 * 2 + pp
                        W2 = range(2)
                        pqs, sss = [], []
                        for which in W2:
                            pq = nps()
                            pqs.append(pq)
                            for kc in range(8):
                                P.mm(pq.v, wqk[which][:, kc, pp * 128:(pp + 1) * 128], HT[:, kc, :], start=(kc == 0), stop=(kc == 7))
                        for which in W2:
                            P.cp(QR[which].v, pqs[which].v, eng="act")
                        for which in W2:
                            P.act(SQ[which].v, QR[which].v, AF.Square)
                        for which in W2:
                            ss = nps()
                            sss.append(ss)
                            P.mm(ss.v, bones_b, SQ[which].v)
                        P.act(RS[0].v, sss[0].v, AF.Ln, bias=EPS64, scale=1.0)
                        P.act(RS[1].v, sss[1].v, AF.Ln, bias=EPS6, scale=1.0 / 64)
                        for which in W2:
                            P.act(RS[which].v, RS[which].v, AF.Exp, scale=-0.5)
                        for which in W2:
                            dst = QT[:, p, :] if which == 0 else KT[:, p, t0:t0 + TT]
                            P.stt(dst, QR[which].v, PV[:, pv0 + 115 + which:pv0 + 116 + which], RS[which].v, ALU.mult, ALU.mult)
                for vp in range(2):
                    wt = wload(w1024(w_in_d, l, 1792 + 1024 + vp * 256, 256), 8, 256)
                    for tb in range(4):
                        pvv = nps()
                        for kc in range(8):
                            P.mm(pvv[:, 0:256], HT[:, kc, tb * 128:(tb + 1) * 128], wt[:, kc, :], start=(kc == 0), stop=(kc == 7))
                        P.cp(VC[:, ti * 4 + tb, vp * 256:(vp + 1) * 256], pvv[:, 0:256], eng=("act" if tb % 2 else "dve"))
                G = 4
                EZ = [P.sb(ph, [128, TT], F32, "EZ") for _ in range(G)]
                SP = [[P.sb(ph, [128, TT], BF16, "SP") for _ in range(G)] for _ in range(2)]
                ATb = [P.sb(ph, [128, TT], BF16, "ATb") for _ in range(G)]
                RRW = [P.sb(ph, [128, TT], F32, "RRW") for _ in range(G)]
                RHI = [P.sb(ph, [128, TT], BF16, "RHI") for _ in range(G)]
                nb = (t0 + TT) // 128
                zc = [0, 0]

                def bank(kind):
                    i = zc[kind] % 2
                    zc[kind] += 1
                    return PS[kind * 2 + i]

                def geom(kb):
                    bl = kb - ti * 4
                    cs = max(0, bl) * 128
                    return bl, cs, TT - cs

                def group(ps, mms):
                    n = len(mms)
                    for j, (o, a_, b_) in enumerate(mms):
                        P.mm(o, a_, b_, start=(j == 0), stop=(j == n - 1))

                def stage_a(grp, kb, slot):
                    bl, cs, ncol = geom(kb)
                    for i in range(G):
                        h = grp * G + i
                        p, b = h // 2, (h % 2) * 64
                        pz = bank(0)
                        mms = [(pz[:, 0:ncol], KT[b:b + 64, p, kb * 128:(kb + 1) * 128], QT[b:b + 64, p, cs:TT])]
                        if bl >= 0:
                            mms.append((pz[:, 0:128], ident_b, negm_b))
                        group(pz, mms)
                        P.act(EZ[i][:, 0:ncol], pz[:, 0:ncol], AF.Exp)
                    for i in range(G):
                        P.act(SP[slot][i][:, 0:ncol], EZ[i][:, 0:ncol], AF.Ln, bias=ONE)

                def stage_b(grp, kb, slot):
                    bl, cs, ncol = geom(kb)
                    first = (kb == nb - 1)
                    for i in range(G):
                        h = grp * G + i
                        p, b = h // 2, (h % 2) * 64
                        pzb = bank(1)
                        mms = [(pzb[:, 0:ncol], KT[b:b + 64, p, kb * 128:(kb + 1) * 128], QT[b:b + 64, p, cs:TT]),
                               (pzb[:, 0:ncol], ntri_b, SP[slot][i][:, 0:ncol])]
                        if not first:
                            mms.append((pzb[:, 0:ncol], nones_b, RHI[i][:, cs:TT]))
                        if bl >= 0:
                            mms.append((pzb[:, 0:128], ident_b, negm_b))
                        group(pzb, mms)
                        P.act(ATb[i][:, cs:TT], pzb[:, 0:ncol], AF.Exp)
                    if kb > 0:
                        for i in range(G):
                            P.tt(RRW[i][:, cs:TT], RRW[i][:, cs:TT], SP[slot][i][:, 0:ncol], ALU.add)
                            P.cp(RHI[i].v, RRW[i].v, eng=("act" if i == 3 else "dve"))
                    for i in range(G):
                        h = grp * G + i
                        P.mm(PS[4 + i][0:64, :], VC[:, kb, h * 64:(h + 1) * 64], ATb[i].v, start=first, stop=(kb == 0))

                for grp in range(8 // G):
                    for i in range(G):
                        P.memset(RRW[i].v, 0.0)
                        P.memset(RHI[i].v, 0.0, eng="pool")
                        P.memset(ATb[i].v, 0.0, eng="pool")
                    kbs = list(range(nb - 1, -1, -1))
                    stage_a(grp, kbs[0], 0)
                    for k, kb in enumerate(kbs):
                        if k + 1 < len(kbs):
                            stage_a(grp, kbs[k + 1], (k + 1) % 2)
                        stage_b(grp, kb, k % 2)
                    for i in range(G):
                        h = grp * G + i
                        P.cp(YB[:, h, :], PS[4 + i][0:64, :], eng=("act" if i % 2 else "dve"))
                P.barrier()
            if l == 0 and ti == 0:
                dump("yb", YB.v)

            with contextlib.ExitStack() as ph:
                QR = [P.sb(ph, [128, TT], F32, "QRm") for _ in range(2)]
                SQ = [P.sb(ph, [128, TT], BF16, "SQm2") for _ in range(2)]
                RS = [P.sb(ph, [128, TT], F32, "RSm2") for _ in range(2)]
                QM = [P.sb(ph, [128, TT], BF16, "QM") for _ in range(2)]
                ES = [[P.sb(ph, [128, TT], BF16, "ES") for _ in range(2)] for _ in range(2)]
                for hp2 in range(2):
                    wt = wload(w1024(w_in_d, l, 3328 + hp2 * 256, 256), 8, 256)
                    R2 = range(2)
                    pqs = []
                    for pp in R2:
                        pq = nps()
                        pqs.append(pq)
                        for kc in range(8):
                            P.mm(pq.v, wt[:, kc, pp * 128:(pp + 1) * 128], HT[:, kc, :], start=(kc == 0), stop=(kc == 7))
                    for pp in R2:
                        P.cp(QR[pp].v, pqs[pp].v, eng="act")
                    for pp in R2:
                        P.act(SQ[pp].v, QR[pp].v, AF.Square)
                    sss = []
                    for pp in R2:
                        ss = nps()
                        sss.append(ss)
                        P.mm(ss.v, ones_b, SQ[pp].v)
                    for pp in R2:
                        P.act(RS[pp].v, sss[pp].v, AF.Ln, bias=EPS128, scale=1.0)
                    for pp in R2:
                        P.act(RS[pp].v, RS[pp].v, AF.Exp, scale=-0.5)
                    for pp in R2:
                        P.stt(QM[pp].v, QR[pp].v, PV[:, pv0 + 117:pv0 + 118], RS[pp].v, ALU.mult, ALU.mult)
                    for mb in range(2):
                        psl = []
                        for pp in R2:
                            h = hp2 * 2 + pp
                            pss = nps()
                            psl.append(pss)
                            P.mm(pss.v, MK[:, h, mb * 128:(mb + 1) * 128], QM[pp].v)
                        for pp in R2:
                            P.act(ES[pp][mb].v, psl[pp].v, AF.Exp)
                    pds, pos = [], []
                    for pp in R2:
                        h = hp2 * 2 + pp
                        pd = nps()
                        pds.append(pd)
                        P.mm(pd.v, ones_b, ES[pp][0].v, start=True, stop=False)
                        P.mm(pd.v, ones_b, ES[pp][1].v, start=False, stop=True)
                        po = nps()
                        pos.append(po)
                        P.mm(po.v, MV[:, 0, h * 128:(h + 1) * 128], ES[pp][0].v, start=True, stop=False)
                        P.mm(po.v, MV[:, 1, h * 128:(h + 1) * 128], ES[pp][1].v, start=False, stop=True)
                    for pp in R2:
                        P.act(RS[pp].v, pds[pp].v, AF.Ln)
                    for pp in R2:
                        P.act(RS[pp].v, RS[pp].v, AF.Exp, scale=-1.0)
                    for pp in R2:
                        P.tt(YM[:, hp2 * 2 + pp, :], pos[pp].v, RS[pp].v, ALU.mult)
                P.barrier()
            if l == 0 and ti == 0:
                dump("ym", YM.v)

            with contextlib.ExitStack() as ph:
                MG = P.sb(ph, [128, 8, TT], BF16, "MG")
                SGm = [P.sb(ph, [128, TT], F32, "SGm") for _ in range(3)]
                WB = [[P.sb(ph, [128, 8, 256], BF16, f"WB{i}") for i in range(3)] for _ in range(2)]
                AC = [[P.sb(ph, [128, TT], F32, "AC") for _ in range(3)] for _ in range(2)]
                for d2 in range(4):
                    wb = WB[d2 % 2]
                    for b in range(2):
                        P.dma("pool", wb[b][0:64, :, :], w_br_d[l][b].re("(h p) n -> p h n", p=64)[:, :, d2 * 256:(d2 + 1) * 256])
                    P.dma("pool", wb[2][:, 0:4, :], w_br_d[l][2].re("(c p) n -> p c n", p=128)[:, :, d2 * 256:(d2 + 1) * 256])
                    gw = [wload(w1024(w_in_d, l, 3840 + b * 1024 + d2 * 256, 256), 8, 256) for b in range(3)]
                    for dd in range(2):
                        d = d2 * 2 + dd
                        ac = AC[d % 2]
                        cols = slice(dd * 128, (dd + 1) * 128)
                        for b in range(3):
                            pg = nps()
                            for kc in range(8):
                                P.mm(pg.v, gw[b][:, kc, cols], HT[:, kc, :], start=(kc == 0), stop=(kc == 7))
                            P.act(SGm[b].v, pg.v, AF.Sigmoid)
                            pb = nps()
                            if b < 2:
                                ysrc = YA if b == 0 else YB
                                for h in range(8):
                                    P.mm(pb.v, wb[b][0:64, h, cols], ysrc[:, h, :], start=(h == 0), stop=(h == 7))
                            else:
                                for c4 in range(4):
                                    P.mm(pb.v, wb[2][:, c4, cols], YM[:, c4, :], start=(c4 == 0), stop=(c4 == 3))
                            P.tt(ac[b].v, SGm[b].v, pb.v, ALU.mult)
                        P.tt(ac[0].v, ac[0].v, ac[1].v, ALU.add)
                        P.tt(MG[:, d, :], ac[0].v, ac[2].v, ALU.add)
                if l == 0 and ti == 0:
                    dump("merged", MG.v)
                for d2 in range(4):
                    wt = wload(w1024(w_out_d, l, d2 * 256, 256), 8, 256)
                    for dd in range(2):
                        d = d2 * 2 + dd
                        px = nps()
                        for kc in range(8):
                            P.mm(px.v, wt[:, kc, dd * 128:(dd + 1) * 128], MG[:, kc, :], start=(kc == 0), stop=(kc == 7))
                        P.tt(XT[:, d, :], XT[:, d, :], px.v, ALU.add)
                P.barrier()
            if l == 0 and ti == 0:
                dump("xattn", XT.v)

            with contextlib.ExitStack() as ph:
                rmsnorm(ph, XT, 8, pv0 + 8, HT, TT)
                ACTB = P.sb(ph, [128, 22, TT], BF16, "ACTB")
                WD = [P.sb(ph, [128, 22, 128], BF16, f"WD{i}") for i in range(2)]
                HB = [P.sb(ph, [128, TT + 2], BF16, "HB") for _ in range(4)]
                DG = [P.sb(ph, [128, 3, 128], BF16, "DG") for _ in range(4)]
                ACg = [P.sb(ph, [128, TT], F32, "ACg") for _ in range(3)]
                ACv = [P.sb(ph, [128, TT], F32, "ACv") for _ in range(3)]
                specs = []
                for c2 in range(11):
                    for dd in range(2):
                        specs.append(("g", c2, dd))
                        specs.append(("v", c2, dd))
                wcur = {}
                stt_ = {}

                def ffn_up(k):
                    kind, c2, dd = specs[k]
                    c = c2 * 2 + dd
                    ci = c if kind == "g" else 22 + c
                    if dd == 0:
                        wcur[kind] = wload(w1024(w_up_d, l, (0 if kind == "g" else DFF) + c2 * 256, 256), 8, 256)
                    wt = wcur[kind]
                    pu = nps()
                    for kc in range(8):
                        P.mm(pu.v, wt[:, kc, dd * 128:(dd + 1) * 128], HT[:, kc, :], start=(kc == 0), stop=(kc == 7))
                    hb = HB[k % 4]
                    dg = DG[k % 4]
                    for j in range(3):
                        wcol = PV[:, pv0 + 163 + 44 * j + ci:pv0 + 164 + 44 * j + ci]
                        P.act(dg[:, j, :], ident, AF.Identity, scale=wcol)
                    P.cp(hb[:, 0:2], HALO[:, ci, :])
                    P.cp(hb[:, 2:TT + 2], pu.v, eng="act")
                    P.cp(HALO[:, ci, :], hb[:, TT:TT + 2])
                    stt_[k] = (hb, dg, ci, c, kind)

                def ffn_conv(k):
                    hb, dg, ci, c, kind = stt_.pop(k)
                    acc = (ACg if kind == "g" else ACv)[c % 3]
                    pc = nps()
                    for j in range(3):
                        P.mm(pc.v, dg[:, j, :], hb[:, j:j + TT], start=(j == 0), stop=(j == 2))
                    P.act(acc.v, pc.v, AF.Silu if kind == "g" else AF.Identity, bias=PV[:, pv0 + 119 + ci:pv0 + 120 + ci])
                    if kind == "v":
                        P.tt(ACTB[:, c, :], ACg[c % 3].v, ACv[c % 3].v, ALU.mult)

                ffn_up(0)
                ffn_up(1)
                for k in range(len(specs)):
                    if k + 2 < len(specs):
                        ffn_up(k + 2)
                    ffn_conv(k)
                for d in range(8):
                    wd = WD[st["wd"] % 2]
                    st["wd"] += 1
                    src = w_dn_d[l].re("(c p) n -> p c n", p=128)[:, :, d * 128:(d + 1) * 128]
                    for a, bnd in ((0, 6), (6, 12), (12, 17), (17, 22)):
                        P.dma("pool", wd[:, a:bnd, :], src[:, a:bnd, :])
                    px = nps()
                    for c in range(22):
                        P.mm(px.v, wd[:, c, :], ACTB[:, c, :], start=(c == 0), stop=(c == 21))
                    P.tt(XT[:, d, :], XT[:, d, :], px.v, ALU.add)
                for a in range(0, 8, 4):
                    P.dma("sp", xdst.v.re("(c p) n -> p c n", p=128)[:, a:a + 4, t0:t0 + TT], XT[:, a:a + 4, :],
                          semtile=XT, is_output=last)
                P.barrier()
    P.emit()
    root.close()
    nc._marks = P.marks
    return nc


def _consts():
    c = np.zeros((128, NCONST), np.float32)
    c[:, 0:128] = np.eye(128)
    c[:, 128:256] = 1.0
    bo = np.zeros((128, 128), np.float32)
    bo[0:64, 0:64] = 1.0
    bo[64:128, 64:128] = 1.0
    c[:, 256:384] = bo
    s = np.arange(64)[:, None]
    t = np.arange(64)[None, :]
    c[0:64, 384:448] = (s < t)
    c[0:64, 448:512] = (s <= t)
    c[0:64, 512:576] = (s > t)
    c[0:64, 576:640] = (s == t)
    s = np.arange(128)[:, None]
    t = np.arange(128)[None, :]
    c[:, 640:768] = (s < t)
    rm = np.ones((64, RT), np.float32)
    rm[:, 0::C] = 0.0
    c[0:64, 768:768 + RT] = rm
    c[:, 896:1024] = (s >= t)
    c[:, 1024] = 1e-6
    c[:, 1025] = 64e-6
    c[:, 1026] = 128e-6
    c[:, 1027] = 64e-5
    c[:, 1028] = 1.0
    c[:, 1029] = 1e-18
    c[0, 1152:1280] = 1.0
    c[:, 1280:1408] = -(s >= t).astype(np.float32)
    c[:, 1408:1536] = -1.0
    c[:, 1536:1664] = -30000.0 * (s >= t)
    c[0:64, 1664:1728] = np.eye(64) - 1.0 / 64
    return c


def _pack(inp):
    pv = np.zeros((128, DEPTH * PVS), np.float32)
    sm = np.zeros((DEPTH, 128, SMW), np.float32)
    for l in range(DEPTH):
        o = l * PVS

        def fm(v, n):
            return np.asarray(v, np.float32).reshape(n, 128).T

        def hm(v):
            return np.asarray(v, np.float32).reshape(8, 64).T

        pv[:, o + 0:o + 8] = fm(inp["norm1_g"][l], 8)
        pv[:, o + 8:o + 16] = fm(inp["norm2_g"][l], 8)
        pv[:, o + 16:o + 24] = fm(inp["mem_norm_g"][l], 8)
        mu = np.asarray(inp["shift_mu"][l], np.float32)
        pv[0:64, o + 24:o + 32] = hm(mu[0:512])
        pv[0:64, o + 32:o + 40] = hm(mu[512:1024])
        pv[0:64, o + 40:o + 48] = hm(mu[1024:1536])
        pv[0:64, o + 48] = mu[1536:1600]
        pv[0:64, o + 49] = mu[1600:1664]
        pv[:, o + 50] = mu[1664:1792]
        pv[0:64, o + 51:o + 59] = hm(inp["decay_w0"][l])
        pv[0:64, o + 59:o + 67] = hm(inp["iclr_a0"][l])
        pv[0:64, o + 67:o + 75] = hm(inp["k_k"][l])
        pv[0:64, o + 75:o + 83] = hm(inp["k_a"][l])
        pv[0:64, o + 83:o + 91] = hm(np.asarray(inp["r_k"][l]).reshape(-1))
        pv[0:64, o + 91:o + 99] = hm(inp["lnx_g"][l])
        pv[0:64, o + 99:o + 107] = hm(inp["lnx_b"][l])
        if l > 0:
            pv[0:64, o + 107:o + 115] = hm(inp["vres_v0"][l - 1])
        pv[:, o + 115] = np.tile(np.asarray(inp["sb_q_norm_g"][l], np.float32), 2)
        pv[:, o + 116] = np.tile(np.asarray(inp["sb_k_norm_g"][l], np.float32), 2)
        pv[:, o + 117] = inp["mem_q_norm_g"][l]
        pv[:, o + 118] = inp["mem_k_norm_g"][l]
        pv[:, o + 119:o + 163] = fm(inp["conv_b"][l], 44)
        cw = np.asarray(inp["conv_w"][l], np.float32)
        pv[:, o + 163:o + 207] = fm(cw[0], 44)
        pv[:, o + 207:o + 251] = fm(cw[1], 44)
        pv[:, o + 251:o + 295] = fm(cw[2], 44)
        sm[l, 0:64, 0:512] = inp["decay_w2"][l]
        sm[l, 0:64, 512:1024] = inp["iclr_a2"][l]
        sm[l, :, 1024:1536] = inp["gate_g2"][l]
        if l > 0:
            v1 = np.asarray(inp["vres_v1"][l - 1], np.float32)
            sm[l, 0:64, 1536:1536 + 256] = v1.reshape(8, 64, 32).transpose(1, 0, 2).reshape(64, 256)
    return pv, sm


def _pack2(inp, sm):
    for l in range(1, DEPTH):
        sm[l, 0:32, 1792:1792 + 512] = inp["vres_v2"][l - 1]
    return sm


_CACHE = {}


def kernel(**inputs):
    inp = {k: np.asarray(v) for k, v in inputs.items()}
    x = inp["x"].astype(np.float32, copy=False)
    mem = inp["mem"].astype(np.float32, copy=False)
    pv, sm = _pack(inp)
    sm = _pack2(inp, sm)
    consts = _consts()
    if "nc" not in _CACHE:
        _CACHE["nc"] = build()
    nc = _CACHE["nc"]
    shared = {
        "pvec": pv, "smat": sm, "consts": consts,
        "w_in": np.ascontiguousarray(inp["w_in"], np.float32),
        "w_mem_kv": np.ascontiguousarray(inp["w_mem_kv"], np.float32),
        "w_branch": np.ascontiguousarray(inp["w_branch"], np.float32),
        "w_out": np.ascontiguousarray(inp["w_out"], np.float32),
        "w_up": np.ascontiguousarray(inp["w_up"], np.float32),
        "w_down": np.ascontiguousarray(inp["w_down"], np.float32),
    }
    in_maps = []
    for b in range(8):
        m = dict(shared)
        m["xT"] = np.ascontiguousarray(x[b].T)
        m["memT"] = np.ascontiguousarray(mem[b].T)
        in_maps.append(m)
    res = run_bass_kernel_spmd(nc, in_maps, core_ids=list(range(8)))
    out = np.stack([np.ascontiguousarray(res.results[b]["yT"].T) for b in range(8)], axis=0)
    return out.astype(np.float32)
```

```python
import math
import os
import contextlib
import numpy as np
import concourse.bass as bass
import concourse.mybir as mybir
from concourse.bass_utils import run_bass_kernel_spmd

F32 = mybir.dt.float32
F32R = mybir.dt.float32r
BF16 = mybir.dt.bfloat16
AF = mybir.ActivationFunctionType
ALU = mybir.AluOpType

D = 1024
T = 2048
TT = 512
NT = T // TT
RT = 128
C = 64
DFF = 2816
INC = 6912
PVS = 320
NCONST = 1728
SMW = 2304
DEPTH = 2
C0 = math.exp(-0.5)

DEBUG = {}


class Res:
    __slots__ = ("name", "last_w", "readers")

    def __init__(self, name):
        self.name = name
        self.last_w = None
        self.readers = {}


class V:
    __slots__ = ("t", "ap")

    def __init__(self, t, ap):
        self.t = t
        self.ap = ap

    def __getitem__(self, k):
        return V(self.t, self.ap[k])

    def bc(self, shape):
        return V(self.t, self.ap.to_broadcast(list(shape)))

    def cast(self, dt):
        return V(self.t, self.ap.bitcast(dt))

    def re(self, pat, **kw):
        return V(self.t, self.ap.rearrange(pat, **kw))


class Tl(Res):
    def __init__(self, h, name, ap=None):
        Res.__init__(self, name)
        self.h = h
        self._ap = ap
        self.dma_sem = None
        self.dma_cnt = 0

    def __getitem__(self, k):
        base = self._ap if self._ap is not None else self.h
        return V(self, base[k])

    @property
    def v(self):
        return self[:]


ENGS = ["pe", "dve", "act", "pool", "sp"]


class Prog:
    def __init__(self, nc):
        self.nc = nc
        self.q = {e: [] for e in ENGS}
        self.sem = {e: nc.alloc_semaphore(f"prog_{e}") for e in ENGS}
        self.cnt = {e: 0 for e in ENGS}
        self.known = {e: {} for e in ENGS}
        self.n = 0
        self.out_tokens = []
        self.dma_tokens = {}
        self.marks = []
        self.pending_pool = None
        self.ghost = {}
        self.sem_pool = []
        self.dma_tiles = []

    def sb(self, stack, shape, dtype=F32, name="t"):
        self.n += 1
        h = stack.enter_context(self.nc.sbuf_tensor(f"{name}_{self.n}", list(shape), dtype))
        t = Tl(h, f"{name}_{self.n}")
        t.persist = getattr(stack, "is_root", False)
        t.readers = dict(self.ghost)
        return t

    def ps(self, stack, shape, dtype=F32, name="p"):
        self.n += 1
        h = stack.enter_context(self.nc.psum_tensor(f"{name}_{self.n}", list(shape), dtype))
        return Tl(h, f"{name}_{self.n}")

    def _collect(self, eng, reads, writes, sync_same):
        deps = {}

        def add(tok):
            if tok is None:
                return
            s, v = tok
            if deps.get(s, 0) < v:
                deps[s] = v

        for t in reads:
            add(t.last_w)
        for t in writes:
            add(t.last_w)
            for s, v in t.readers.items():
                add((s, v))
        waits = []
        kn = self.known[eng]
        own = self.sem[eng]
        for s, v in deps.items():
            if s is own and not sync_same:
                continue
            if kn.get(s, 0) >= v:
                continue
            kn[s] = v
            waits.append((s, v))
        return waits

    def _commit(self, tok, reads, writes):
        s, v = tok
        for t in reads:
            if t.readers.get(s, 0) < v:
                t.readers[s] = v
        for t in writes:
            t.last_w = tok
            t.readers = {}

    def op(self, eng, fn, reads=(), writes=(), sync_same=True):
        if eng == "pool":
            self._flush_pool(list(reads) + list(writes))
        waits = self._collect(eng, reads, writes, sync_same)
        self.cnt[eng] += 1
        tok = (self.sem[eng], self.cnt[eng])
        self.q[eng].append((waits, fn, tok, 1))
        self._commit(tok, reads, writes)
        return tok

    def dma(self, q, out, in_, semtile=None, is_output=False):
        reads = [in_.t]
        writes = [out.t]
        if q == "pool":
            self._flush_pool(reads + writes)
        waits = self._collect(q, reads, writes, True)
        if semtile is None:
            semtile = out.t if isinstance(out.t, Tl) and out.t.h is not None and not getattr(out.t, "is_dram", False) else in_.t
        if semtile.dma_sem is None:
            if self.sem_pool:
                semtile.dma_sem, semtile.dma_cnt = self.sem_pool.pop()
            else:
                semtile.dma_sem = self.nc.alloc_semaphore(f"dma_{semtile.name}")
            self.dma_tiles.append(semtile)
        semtile.dma_cnt += 16
        tok = (semtile.dma_sem, semtile.dma_cnt)
        oa, ia = out.ap, in_.ap
        self.q[q].append((waits, lambda e: e.dma_start(out=oa, in_=ia), tok, 16))
        self._commit(tok, reads, writes)
        self.dma_tokens[semtile.dma_sem] = tok
        if is_output:
            self.out_tokens.append(tok)
        return tok

    def _flush_pool(self, res):
        if self.pending_pool is None or all(getattr(t, "persist", False) for t in res):
            return
        waits = []
        kn = self.known["pool"]
        for s, v in self.pending_pool:
            if kn.get(s, 0) < v:
                kn[s] = v
                waits.append((s, v))
        if waits:
            self.q["pool"].append((waits, None, None, 0))
        self.pending_pool = None
        self.ghost = {}

    def mark(self, name):
        self.marks.append((name, dict(self.cnt)))

    def barrier(self):
        toks = [(self.sem[e], self.cnt[e]) for e in ENGS if self.cnt[e] > 0]
        toks += list(self.dma_tokens.values())
        g = {}
        for s_, v in toks:
            if g.get(s_, 0) < v:
                g[s_] = v
        self.ghost = g
        keep = []
        for t in self.dma_tiles:
            if getattr(t, "persist", False):
                keep.append(t)
            else:
                self.sem_pool.append((t.dma_sem, t.dma_cnt))
                t.dma_sem = None
        self.dma_tiles = keep

    def mm(self, out, lhsT, rhs, start=True, stop=True):
        oa, la, ra = out.ap, lhsT.ap, rhs.ap
        self.op("pe", lambda e: e.matmul(oa, lhsT=la, rhs=ra, start=start, stop=stop),
                reads=[lhsT.t, rhs.t], writes=[out.t], sync_same=False)

    def tr(self, out, in_, ident):
        oa, ia, da = out.ap, in_.ap, ident.ap
        self.op("pe", lambda e: e.transpose(oa, ia, da), reads=[in_.t, ident.t], writes=[out.t], sync_same=False)

    def act(self, out, in_, func, bias=None, scale=None):
        oa, ia = out.ap, in_.ap
        reads = [in_.t]
        kw = {}
        if bias is not None:
            if isinstance(bias, V):
                kw["bias"] = bias.ap
                reads.append(bias.t)
            else:
                kw["bias"] = bias
        if scale is not None:
            if isinstance(scale, V):
                kw["scale"] = scale.ap
                reads.append(scale.t)
            else:
                kw["scale"] = scale
        self.op("act", lambda e: e.activation(out=oa, in_=ia, func=func, **kw), reads=reads, writes=[out.t])

    def tt(self, out, in0, in1, op, eng="dve"):
        oa, a, b = out.ap, in0.ap, in1.ap
        self.op(eng, lambda e: e.tensor_tensor(out=oa, in0=a, in1=b, op=op), reads=[in0.t, in1.t], writes=[out.t])

    def ts(self, out, in0, s1, op0, s2=None, op1=None, eng="dve"):
        oa, a = out.ap, in0.ap
        reads = [in0.t]
        if isinstance(s1, V):
            reads.append(s1.t)
            s1 = s1.ap
        if isinstance(s2, V):
            reads.append(s2.t)
            s2 = s2.ap
        if op1 is None:
            self.op(eng, lambda e: e.tensor_scalar(out=oa, in0=a, scalar1=s1, scalar2=None, op0=op0), reads=reads, writes=[out.t])
        else:
            self.op(eng, lambda e: e.tensor_scalar(out=oa, in0=a, scalar1=s1, scalar2=s2, op0=op0, op1=op1), reads=reads, writes=[out.t])

    def stt(self, out, in0, scalar, in1, op0, op1):
        oa, a, b = out.ap, in0.ap, in1.ap
        reads = [in0.t, in1.t]
        if isinstance(scalar, V):
            reads.append(scalar.t)
            scalar = scalar.ap
        self.op("dve", lambda e: e.scalar_tensor_tensor(out=oa, in0=a, scalar=scalar, in1=b, op0=op0, op1=op1),
                reads=reads, writes=[out.t])

    def cp(self, out, in_, eng="dve"):
        oa, ia = out.ap, in_.ap
        if eng == "act":
            self.op("act", lambda e: e.copy(out=oa, in_=ia), reads=[in_.t], writes=[out.t])
        else:
            self.op(eng, lambda e: e.tensor_copy(out=oa, in_=ia), reads=[in_.t], writes=[out.t])

    def recip(self, out, in_):
        oa, ia = out.ap, in_.ap
        self.op("dve", lambda e: e.reciprocal(out=oa, in_=ia), reads=[in_.t], writes=[out.t])

    def scan(self, out, d0, d1, init, op0, op1):
        oa, a, b = out.ap, d0.ap, d1.ap
        self.op("dve", lambda e: e.tensor_tensor_scan(out=oa, data0=a, data1=b, initial=init, op0=op0, op1=op1),
                reads=[d0.t, d1.t], writes=[out.t])

    def memset(self, out, val, eng="dve"):
        oa = out.ap
        self.op(eng, lambda e: e.memset(oa, val), reads=[], writes=[out.t])

    def emit(self):
        nc = self.nc
        final = {}
        for s, v in self.out_tokens:
            if final.get(s, 0) < v:
                final[s] = v
        final_waits = list(final.items())
        q = self.q

        def run(e, lst, extra=()):
            for waits, fn, tok, inc in lst:
                for s, v in waits:
                    e.wait_ge(s, v)
                if fn is not None:
                    fn(e).then_inc(tok[0], inc)
            for s, v in extra:
                e.wait_ge(s, v)

        with nc.Block() as block:
            @block.tensor
            def _(e):
                run(e, q["pe"])

            @block.vector
            def _(e):
                run(e, q["dve"])

            @block.scalar
            def _(e):
                run(e, q["act"])

            @block.gpsimd
            def _(e):
                run(e, q["pool"])

            @block.sync
            def _(e):
                run(e, q["sp"], final_waits)


def build(n_layers=DEPTH, n_tiles=NT, dbg=None):
    nc = bass.Bass("TRN2", target_bir_lowering=False)
    P = Prog(nc)

    def dram(name, shape, kind, dt=F32):
        h = nc.dram_tensor(name, list(shape), dt, kind=kind)
        t = Tl(None, name, ap=h.ap())
        t.is_dram = True
        t.persist = True
        return t

    xT_d = dram("xT", [D, T], "ExternalInput")
    memT_d = dram("memT", [D, 256], "ExternalInput")
    pvec_d = dram("pvec", [128, DEPTH * PVS], "ExternalInput")
    smat_d = dram("smat", [DEPTH, 128, SMW], "ExternalInput")
    const_d = dram("consts", [128, NCONST], "ExternalInput")
    w_in_d = dram("w_in", [DEPTH, D, INC], "ExternalInput")
    w_kv_d = dram("w_mem_kv", [DEPTH, D, 1024], "ExternalInput")
    w_br_d = dram("w_branch", [DEPTH, 3, 512, D], "ExternalInput")
    w_out_d = dram("w_out", [DEPTH, D, D], "ExternalInput")
    w_up_d = dram("w_up", [DEPTH, D, 2 * DFF], "ExternalInput")
    w_dn_d = dram("w_down", [DEPTH, 8, 128, 22 * 128], "ExternalInput")
    yT_d = dram("yT", [D, T], "ExternalOutput")
    xs_d = dram("xs", [D, T], "Internal")
    vf_d = dram("vf", [8, 64, T], "Internal")
    dbg_d = {}
    if dbg:
        for k, shp in dbg.items():
            dbg_d[k] = dram("dbg_" + k, shp, "ExternalOutput")

    root = contextlib.ExitStack()
    root.is_root = True
    CT = P.sb(root, [128, NCONST], F32, "CT")
    CB = P.sb(root, [128, 1024], BF16, "CB")
    RM8 = P.sb(root, [64, 8, RT], BF16, "RM8")
    PV = P.sb(root, [128, DEPTH * PVS], F32, "PV")
    OMM = P.sb(root, [128, DEPTH * 32], F32, "OMM")
    SM = P.sb(root, [128, SMW], F32, "SM")
    SMb = P.sb(root, [128, 1536], BF16, "SMb")
    KT = P.sb(root, [128, 4, T], BF16, "KT")
    VC = P.sb(root, [128, T // 128, 512], BF16, "VC")
    MK = P.sb(root, [128, 4, 256], BF16, "MK")
    MV = P.sb(root, [128, 2, 512], BF16, "MV")
    XT = P.sb(root, [128, 8, TT], F32, "XT")
    HT = P.sb(root, [128, 8, TT], BF16, "HT")
    NWB = 6
    WT = [P.sb(root, [128, 8, 256], BF16, f"WT{i}") for i in range(NWB)]
    Hst = P.sb(root, [64, 8, 64], F32, "Hst")
    Hb = P.sb(root, [64, 8, 64], BF16, "Hb")
    PREV = P.sb(root, [128, 32], F32, "PREV")
    HALO = P.sb(root, [128, 44, 2], F32, "HALO")
    YA = P.sb(root, [64, 8, TT], BF16, "YA")
    YB = P.sb(root, [64, 8, TT], BF16, "YB")
    YM = P.sb(root, [128, 4, TT], BF16, "YM")
    PS = [P.ps(root, [128, 512], F32, f"PS{i}") for i in range(8)]
    st = {"ps": 0, "w": 0, "wd": 0}

    def nps(k=6):
        i = st["ps"] % k
        st["ps"] += 1
        return PS[i]

    ident = CT[:, 0:128]
    ones = CT[:, 128:256]
    bones = CT[:, 256:384]
    su = CT[0:64, 384:448]
    iu = CT[0:64, 448:512]
    sl = CT[0:64, 512:576]
    id64 = CT[0:64, 576:640]
    smk = CT[:, 640:768]
    rmask = CT[0:64, 768:896]
    tri = CT[:, 896:1024]
    EPS6 = CT[:, 1024:1025]
    EPS64 = CT[:, 1025:1026]
    EPS128 = CT[:, 1026:1027]
    EPSLN = CT[:, 1027:1028]
    ONE = CT[:, 1028:1029]
    EPS18 = CT[:, 1029:1030]
    tri_b = CB[:, 0:128]
    ones_b = CB[:, 128:256]
    Cm = CT[0:64, 1664:1728]
    ident_b = CB[:, 384:512]
    bones_b = CB[:, 896:1024]
    rmask8 = RM8.v.re("p h t -> p (h t)")
    ntri_b = CB[:, 512:640]
    nones_b = CB[:, 640:768]
    negm_b = CB[:, 768:896]
    sel0_b = CB[:, 256:384]

    def m8(mv):
        return V(mv.t, mv.ap.rearrange("p (o c) -> p o c", o=1).to_broadcast([64, 8, 64]))

    P.dma("sp", CT.v, const_d.v)
    P.dma("sp", PV.v, pvec_d.v)
    P.cp(CB[:, 0:128], tri)
    P.cp(CB[:, 128:256], ones)
    P.cp(CB[:, 256:384], CT[:, 1152:1280])
    P.cp(CB[:, 384:512], ident)
    P.cp(CB[:, 512:896], CT[:, 1280:1664])
    P.cp(CB[:, 896:1024], bones)
    P.cp(RM8.v, rmask.re("p (o t) -> p o t", o=1).bc([64, 8, RT]))
    for l in range(DEPTH):
        P.ts(OMM[:, l * 32:l * 32 + 27], PV[:, l * PVS + 24:l * PVS + 51], -1.0, ALU.mult, 1.0, ALU.add)

    def wload(src_rows_view, kc, ncols):
        wt = WT[st["w"] % NWB]
        st["w"] += 1
        half = max(1, kc // 2)
        for a in range(0, kc, half):
            b = min(kc, a + half)
            P.dma("pool", wt[:, a:b, 0:ncols], src_rows_view[:, a:b, :])
        return wt

    def w1024(wd, l, c0, ncols):
        return wd[l].re("(c p) n -> p c n", p=128)[:, :, c0:c0 + ncols]

    def rmsnorm(phase, src, n, gcol0, dst, N):
        SQ = [P.sb(phase, [128, N], BF16, "SQ") for _ in range(4)]
        RS = P.sb(phase, [128, N], F32, "RS")
        ss = nps()
        for c in range(n):
            s = SQ[c % 4]
            P.act(s.v, src[:, c, :], AF.Square)
            P.mm(ss[:, 0:N], ones_b, s.v, start=(c == 0), stop=(c == n - 1))
        P.act(RS.v, ss[:, 0:N], AF.Ln, bias=EPS6, scale=1.0 / (128 * n))
        P.act(RS.v, RS.v, AF.Exp, scale=-0.5)
        for c in range(n):
            P.stt(dst[:, c, :], src[:, c, :], PV[:, gcol0 + c:gcol0 + c + 1], RS.v, ALU.mult, ALU.mult)

    def dump(name, view):
        if dbg and name in dbg_d:
            P.dma("pool", dbg_d[name].v, view)

    for l in range(n_layers):
        pv0 = l * PVS
        last = (l == n_layers - 1)
        xsrc = xT_d if l == 0 else xs_d
        xdst = yT_d if last else xs_d
        P.dma("sp", SM.v, smat_d[l])
        P.cp(SMb.v, SM[:, 0:1536])
        P.memset(Hst.v, 0.0)
        P.memset(Hb.v, 0.0)
        P.memset(PREV.v, 0.0)
        P.memset(HALO.v, 0.0)

        with contextlib.ExitStack() as ph:
            MT = P.sb(ph, [128, 8, 256], F32, "MT")
            MN = P.sb(ph, [128, 8, 256], BF16, "MN")
            KR = P.sb(ph, [128, 256], F32, "KR")
            SQ = P.sb(ph, [128, 256], F32, "SQm")
            RS = P.sb(ph, [128, 256], F32, "RSm")
            P.dma("sp", MT.v, memT_d.v.re("(c p) n -> p c n", p=128))
            rmsnorm(ph, MT, 8, pv0 + 16, MN, 256)
            for hp in range(2):
                wt = wload(w1024(w_kv_d, l, hp * 256, 256), 8, 256)
                for hh in range(2):
                    h = hp * 2 + hh
                    pk = nps()
                    for kc in range(8):
                        P.mm(pk[:, 0:256], wt[:, kc, hh * 128:(hh + 1) * 128], MN[:, kc, :], start=(kc == 0), stop=(kc == 7))
                    P.cp(KR.v, pk[:, 0:256], eng="act")
                    P.act(SQ.v, KR.v, AF.Square)
                    ss = nps()
                    P.mm(ss[:, 0:256], ones, SQ.v)
                    P.act(RS.v, ss[:, 0:256], AF.Ln, bias=EPS6, scale=1.0 / 128)
                    P.act(RS.v, RS.v, AF.Exp, scale=-0.5)
                    P.stt(MK[:, h, :], KR.v, PV[:, pv0 + 118:pv0 + 119], RS.v, ALU.mult, ALU.mult)
            for vp in range(2):
                wt = wload(w1024(w_kv_d, l, 512 + vp * 256, 256), 8, 256)
                for mb in range(2):
                    pvv = nps()
                    for kc in range(8):
                        P.mm(pvv[:, 0:256], MN[:, kc, mb * 128:(mb + 1) * 128], wt[:, kc, :], start=(kc == 0), stop=(kc == 7))
                    P.cp(MV[:, mb, vp * 256:(vp + 1) * 256], pvv[:, 0:256], eng="act")
            P.barrier()

        for ti in range(n_tiles):
            t0 = ti * TT
            with contextlib.ExitStack() as ph:
                for a in range(0, 8, 4):
                    P.dma("sp", XT[:, a:a + 4, :], xsrc.v.re("(c p) n -> p c n", p=128)[:, a:a + 4, t0:t0 + TT])
                rmsnorm(ph, XT, 8, pv0 + 0, HT, TT)
                P.barrier()
            if l == 0 and ti == 0:
                dump("h", HT.v)

            for sub in range(TT // RT):
                q0 = sub * RT
                g0 = t0 + q0
                with contextlib.ExitStack() as ph:
                    GG = P.sb(ph, [64, 8, RT], F32, "GG")
                    EV = P.sb(ph, [64, 8, RT], F32, "EV")
                    GC = P.sb(ph, [64, 8, 2], F32, "GC")
                    RTb, ATb, BTb, KTb, VVb = [P.sb(ph, [64, 8, RT], BF16, n) for n in ("RTb", "ATb", "BTb", "KTb", "VVb")]
                    VTK = P.sb(ph, [64, 2, 512], BF16, "VTK")
                    BTK = P.sb(ph, [64, 2, 512], BF16, "BTK")
                    KTK = P.sb(ph, [64, 2, 512], BF16, "KTK")
                    ph2 = contextlib.ExitStack()

                    def fb(nm):
                        return P.sb(ph2, [64, 8, RT], F32, nm)
                    E1, E2, E3, AA, SG, CU, RR, VV = [fb(n) for n in ("E1", "E2", "E3", "AA", "SG", "CU", "RR", "VV")]
                    PR = [P.sb(ph2, [64, 4, RT + 1], F32, "PR")] * 2
                    TM = [P.sb(ph2, [64, 4, RT], F32, "TM")] * 2
                    WL = P.sb(ph2, [64, RT], BF16, "WL")
                    AL = P.sb(ph2, [64, RT], BF16, "AL")
                    GL = P.sb(ph2, [128, RT + 1], F32, "GL")
                    GS = P.sb(ph2, [128, RT], F32, "GS")
                    GSb = P.sb(ph2, [128, RT], BF16, "GSb")
                    prc = [0]

                    def shift4(psv, nh, prev0, mu0, dst):
                        pom = PR[prc[0] % 2]
                        pmu = TM[prc[0] % 2]
                        prc[0] += 1
                        prevv = PREV[0:64, prev0:prev0 + nh].re("p (h o) -> p h o", o=1)
                        muv = PV[0:64, pv0 + mu0:pv0 + mu0 + nh].re("p (h o) -> p h o", o=1)
                        for hh in range(nh):
                            mu1 = PV[0:64, pv0 + mu0 + hh:pv0 + mu0 + hh + 1]
                            om1 = OMM[0:64, l * 32 + (mu0 - 24) + hh:l * 32 + (mu0 - 24) + hh + 1]
                            P.act(pom[:, hh, 0:RT], psv[:, hh, :], AF.Identity, scale=om1)
                            P.act(pmu[:, hh, 1:RT], psv[:, hh, 0:RT - 1], AF.Identity, scale=mu1)
                        P.tt(pmu[:, 0:nh, 0:1], prevv, muv, ALU.mult)
                        P.cp(prevv, psv[:, :, RT - 1:RT], eng="act")
                        P.tt(dst, pom[:, 0:nh, 0:RT], pmu[:, 0:nh, :], ALU.add)

                    hsl = HT[:, :, q0:q0 + RT]
                    wt = wload(w1024(w_in_d, l, 1536, 256), 8, 256)
                    pa = nps()
                    for j in range(2):
                        for kc in range(8):
                            P.mm(pa[0:64, j * RT:(j + 1) * RT], wt[:, kc, j * 64:(j + 1) * 64], hsl[:, kc, :], start=(kc == 0), stop=(kc == 7))
                    WA = P.sb(ph2, [64, 2, RT], F32, "WA")
                    shift4(pa[0:64, 0:2 * RT].re("p (h t) -> p h t", h=2), 2, 24, 48, WA.v)
                    P.act(WL.v, WA[:, 0, :], AF.Tanh)
                    P.cp(AL.v, WA[:, 1, :])
                    pg = nps()
                    for kc in range(8):
                        P.mm(pg[:, 0:RT], wt[:, kc, 128:256], hsl[:, kc, :], start=(kc == 0), stop=(kc == 7))
                    P.cp(GL[:, 0:1], PREV[:, 26:27])
                    P.cp(GL[:, 1:RT + 1], pg[:, 0:RT], eng="act")
                    P.cp(PREV[:, 26:27], GL[:, RT:RT + 1])
                    P.ts(GS.v, GL[:, 0:RT], PV[:, pv0 + 50:pv0 + 51], ALU.mult)
                    P.stt(GS.v, GL[:, 1:RT + 1], OMM[:, l * 32 + 26:l * 32 + 27], GS.v, ALU.mult, ALU.add)
                    P.act(GSb.v, GS.v, AF.Sigmoid)
                    for hq in range(2):
                        p1 = nps()
                        p2 = nps()
                        p3 = nps()
                        for hh in range(4):
                            h = hq * 4 + hh
                            P.mm(p1[0:64, hh * RT:(hh + 1) * RT], SMb[0:64, h * 64:(h + 1) * 64], WL.v)
                            P.mm(p2[0:64, hh * RT:(hh + 1) * RT], SMb[0:64, 512 + h * 64:512 + (h + 1) * 64], AL.v)
                            P.mm(p3[0:64, hh * RT:(hh + 1) * RT], SMb[:, 1024 + h * 64:1024 + (h + 1) * 64], GSb.v)
                        for hh in range(4):
                            h = hq * 4 + hh
                            P.act(SG[:, h, :], p1[0:64, hh * RT:(hh + 1) * RT], AF.Sigmoid, bias=PV[0:64, pv0 + 51 + h:pv0 + 52 + h])
                            P.act(AA[:, h, :], p2[0:64, hh * RT:(hh + 1) * RT], AF.Sigmoid, bias=PV[0:64, pv0 + 59 + h:pv0 + 60 + h])
                        P.cp(GG[:, hq * 4:hq * 4 + 4, :], p3[0:64, 0:4 * RT].re("p (h t) -> p h t", h=4))
                    P.scan(CU.v.re("p h t -> p (h t)"), rmask8, SG.v.re("p h t -> p (h t)"), 0.0, ALU.mult, ALU.add)
                    P.act(E1.v, CU.v, AF.Exp, scale=-C0)
                    P.act(E2.v, CU.v, AF.Exp, scale=C0)
                    P.tt(SG.v, CU.v, SG.v, ALU.subtract)
                    P.act(E3.v, SG.v, AF.Exp, scale=-C0)
                    P.cp(GC.v, E1.v.re("p h (c t) -> p h c t", c=2)[:, :, :, C - 1])

                    P.mark('rw_lora')
                    def proj_heads(col0, prev0, mu0, dst):
                        for hq in range(2):
                            wt = wload(w1024(w_in_d, l, col0 + hq * 256, 256), 8, 256)
                            pb = nps()
                            for hh in range(4):
                                for kc in range(8):
                                    P.mm(pb[0:64, hh * RT:(hh + 1) * RT], wt[:, kc, hh * 64:(hh + 1) * 64], hsl[:, kc, :], start=(kc == 0), stop=(kc == 7))
                            shift4(pb[0:64, 0:4 * RT].re("p (h t) -> p h t", h=4), 4, prev0 + hq * 4, mu0 + hq * 4, dst[:, hq * 4:hq * 4 + 4, :])

                    proj_heads(0, 0, 24, RR.v)
                    KK = CU
                    proj_heads(512, 8, 32, KK.v)
                    proj_heads(1024, 16, 40, VV.v)

                    P.mark('rw_proj')

                    def pbc(col):
                        return PV[0:64, pv0 + col:pv0 + col + 8].re("p (h o) -> p h o", o=1).bc([64, 8, RT])

                    T1 = SG
                    T2 = P.sb(ph2, [64, 8, RT], F32, "T2")
                    P.tt(T1.v, KK.v, pbc(67), ALU.mult)
                    P.act(VVb.v, T1.v, AF.Square)
                    for hq in range(2):
                        pn = nps()
                        P.mm(pn[0:64, 0:4 * RT], ones_b[0:64, 0:64], VVb[:, hq * 4:hq * 4 + 4, :].re("p h t -> p (h t)"))
                        P.act(T2[:, hq * 4:hq * 4 + 4, :].re("p h t -> p (h t)"), pn[0:64, 0:4 * RT], AF.Ln, bias=EPS18[0:64, :])
                    P.act(T2.v, T2.v, AF.Exp, scale=-0.5)
                    P.tt(T1.v, T1.v, T2.v, ALU.mult)
                    P.stt(ATb.v, T1.v, -1.0, E3.v, ALU.mult, ALU.mult)
                    P.tt(T1.v, T1.v, AA.v, ALU.mult)
                    P.ts(T2.v, AA.v, -1.0, ALU.add)
                    P.tt(T2.v, T2.v, pbc(75), ALU.mult)
                    P.stt(T2.v, T2.v, 1.0, KK.v, ALU.add, ALU.mult)
                    P.tt(AA.v, RR.v, T2.v, ALU.mult)
                    P.tt(RTb.v, AA.v, pbc(83), ALU.mult)
                    for hq in range(2):
                        pn = nps()
                        P.mm(pn[0:64, 0:4 * RT], ones_b[0:64, 0:64], RTb[:, hq * 4:hq * 4 + 4, :].re("p h t -> p (h t)"))
                        P.cp(EV[:, hq * 4:hq * 4 + 4, :].re("p h t -> p (h t)"), pn[0:64, 0:4 * RT], eng="act")
                    P.tt(KTb.v, T2.v, E2.v, ALU.mult)
                    P.tt(BTb.v, T1.v, E2.v, ALU.mult)
                    P.tt(RTb.v, RR.v, E1.v, ALU.mult)
                    KTt, BTt, ATt, RTt = KTb, BTb, ATb, RTb
                    vfv = vf_d.v.re("h p t -> p h t")[:, :, g0:g0 + RT]
                    if l == 0:
                        P.dma("sp", vfv, VV.v)
                    else:
                        VF = RR
                        P.dma("sp", VF.v, vfv)
                        p32 = nps()
                        for h in range(8):
                            P.mm(p32[0:32, 0:RT], SM[0:64, 1536 + h * 32:1536 + (h + 1) * 32], VV[:, h, :], start=(h == 0), stop=(h == 7))
                        T32 = P.sb(ph2, [32, RT], F32, "T32")
                        P.cp(T32.v, p32[0:32, 0:RT], eng="act")
                        for hq in range(2):
                            pz = nps()
                            for hh in range(4):
                                h = hq * 4 + hh
                                P.mm(pz[0:64, hh * RT:(hh + 1) * RT], SM[0:32, 1792 + h * 64:1792 + (h + 1) * 64], T32.v)
                            for hh in range(4):
                                h = hq * 4 + hh
                                P.act(T2[:, h, :], pz[0:64, hh * RT:(hh + 1) * RT], AF.Sigmoid, bias=PV[0:64, pv0 + 107 + h:pv0 + 108 + h])
                        P.tt(T1.v, VF.v, VV.v, ALU.subtract)
                        P.tt(T1.v, T1.v, T2.v, ALU.mult)
                        P.tt(VV.v, VV.v, T1.v, ALU.add)
                    P.tt(EV.v, EV.v, VV.v, ALU.mult)
                    P.mark('rw_prep')
                    P.cp(VVb.v, VV.v, eng="act")
                    for (src, dstk) in ((VVb, VTK), (BTt, BTK), (KTt, KTK)):
                        for c in range(2):
                            pt = nps(8)
                            ptb = pt.v.cast(BF16)
                            for h in range(8):
                                P.tr(ptb[0:64, h * 64:(h + 1) * 64], src[:, h, c * C:(c + 1) * C], ident_b[0:64, 0:64])
                            P.cp(dstk[:, c, :], ptb[0:64, 0:512], eng=("act" if c else "dve"))
                    P.barrier()
                    ph2.close()
                    P.mark('rw_tr')
                    def t64(nm, dt=BF16):
                        return P.sb(ph, [64, 8, 64], dt, nm)
                    CH = []
                    for c in range(2):
                        CH.append(dict(NM=t64("NM"), LM=t64("LM"), N2=t64("N2"), L2=t64("L2"), MAK=t64("MAK"), MRB=t64("MRB"),
                                       MRK=t64("MRK"), PP=t64("PP", F32), PPb=t64("PPb")))
                    WS, US = t64("WS"), t64("US")
                    YSc = [t64("YS", F32) for _ in range(2)]
                    YQc = [t64("YQ", F32) for _ in range(2)]
                    MEc = [t64("MEAN", F32) for _ in range(2)]

                    def gram(dst, a, b, c, mask):
                        pgm = nps(8)
                        for h in range(8):
                            P.mm(pgm[0:64, h * 64:(h + 1) * 64], a[:, h, c * C:(c + 1) * C], b[:, h, c * C:(c + 1) * C])
                        P.tt(dst.v, pgm[0:64, :].re("p (h t) -> p h t", h=8), m8(mask), ALU.mult)

                    def mm8(dst_ps, a, b):
                        for h in range(8):
                            P.mm(dst_ps[0:64, h * 64:(h + 1) * 64], a[:, h, :], b[:, h, :])

                    def v8(ps):
                        return ps[0:64, :].re("p (h t) -> p h t", h=8)

                    for c in range(2):
                        d = CH[c]
                        gram(d["NM"], BTt, ATt, c, su)
                        gram(d["LM"], ATt, BTt, c, sl)
                        gram(d["MAK"], KTt, ATt, c, su)
                        gram(d["MRB"], BTt, RTt, c, iu)
                        gram(d["MRK"], KTt, RTt, c, iu)
                        P.tt(d["PP"].v, d["NM"].v, m8(id64), ALU.add)
                        P.cp(d["PPb"].v, d["PP"].v, eng="act")
                        d["cur"] = (d["LM"], d["NM"], d["L2"], d["N2"])
                    for lev in range(1, 6):
                        pls, pns, pqs = [], [], []
                        for c in range(2):
                            Lk, Nk, Lo, No = CH[c]["cur"]
                            pl = nps(8)
                            mm8(pl, Nk, Lk)
                            pls.append(pl)
                            if lev < 5:
                                pn = nps(8)
                                mm8(pn, Lk, Nk)
                                pns.append(pn)
                        for c in range(2):
                            Lk, Nk, Lo, No = CH[c]["cur"]
                            P.cp(Lo.v, v8(pls[c]), eng="act")
                            if lev < 5:
                                P.cp(No.v, v8(pns[c]))
                            CH[c]["cur"] = (Lo, No, Lk, Nk)
                        for c in range(2):
                            Lk = CH[c]["cur"][0]
                            pq = nps(8)
                            mm8(pq, Lk, CH[c]["PPb"])
                            pqs.append(pq)
                        for c in range(2):
                            P.tt(CH[c]["PP"].v, CH[c]["PP"].v, v8(pqs[c]), ALU.add)
                            P.cp(CH[c]["PPb"].v, CH[c]["PP"].v, eng="act")
                    P.mark('rw_inv')
                    for c in range(2):
                        d = CH[c]
                        MAK, MRB, MRK, PPb = d["MAK"], d["MRB"], d["MRK"], d["PPb"]
                        pw = nps(8)
                        for h in range(8):
                            P.mm(pw[0:64, h * 64:(h + 1) * 64], ATt[:, h, c * C:(c + 1) * C], Hb[:, h, :], start=True, stop=False)
                            P.mm(pw[0:64, h * 64:(h + 1) * 64], MAK[:, h, :], VTK[:, c, h * 64:(h + 1) * 64], start=False, stop=True)
                        P.cp(WS.v, v8(pw), eng="act")
                        pu = nps(8)
                        mm8(pu, PPb, WS)
                        P.cp(US.v, v8(pu), eng="act")
                        py = nps(8)
                        for h in range(8):
                            o = py[0:64, h * 64:(h + 1) * 64]
                            P.mm(o, Hb[:, h, :], RTt[:, h, c * C:(c + 1) * C], start=True, stop=False)
                            P.mm(o, US[:, h, :], MRB[:, h, :], start=False, stop=False)
                            P.mm(o, VTK[:, c, h * 64:(h + 1) * 64], MRK[:, h, :], start=False, stop=True)
                        phh = nps(8)
                        for h in range(8):
                            o = phh[0:64, h * 64:(h + 1) * 64]
                            P.mm(o, BTK[:, c, h * 64:(h + 1) * 64], US[:, h, :], start=True, stop=False)
                            P.mm(o, KTK[:, c, h * 64:(h + 1) * 64], VTK[:, c, h * 64:(h + 1) * 64], start=False, stop=True)
                        P.tt(Hst.v, Hst.v, v8(phh), ALU.add)
                        P.tt(Hst.v, Hst.v, GC[:, :, c:c + 1].bc([64, 8, 64]), ALU.mult)
                        P.cp(Hb.v, Hst.v, eng="act")
                        P.mark('rw_state')
                        P.cp(YSc[c].v, v8(py), eng="act")
                    fl = "p h t -> p (h t)"
                    pcs, pqs2 = [], []
                    for c in range(2):
                        pc = nps(8)
                        P.mm(pc[0:64, :], Cm, YSc[c].v.re(fl))
                        pcs.append(pc)
                    for c in range(2):
                        P.act(YQc[c].v.re(fl), pcs[c][0:64, :], AF.Square)
                    for c in range(2):
                        pq2 = nps(8)
                        P.mm(pq2[0:64, :], ones[0:64, 0:64], YQc[c].v.re(fl))
                        pqs2.append(pq2)
                    for c in range(2):
                        P.act(MEc[c].v.re(fl), pqs2[c][0:64, :], AF.Ln, bias=EPSLN[0:64, :], scale=1.0 / 64)
                    for c in range(2):
                        P.act(MEc[c].v, MEc[c].v, AF.Exp, scale=-0.5)
                    for c in range(2):
                        P.tt(YSc[c].v.re(fl), pcs[c][0:64, :], MEc[c].v.re(fl), ALU.mult)
                    lg = PV[0:64, pv0 + 91:pv0 + 99].re("p (h o) -> p h o", o=1).bc([64, 8, 64])
                    lb = PV[0:64, pv0 + 99:pv0 + 107].re("p (h o) -> p h o", o=1).bc([64, 8, 64])
                    for c in range(2):
                        P.tt(YSc[c].v, YSc[c].v, lg, ALU.mult)
                    for c in range(2):
                        P.tt(YSc[c].v, YSc[c].v, lb, ALU.add)
                    for c in range(2):
                        P.tt(YSc[c].v, YSc[c].v, EV[:, :, c * C:(c + 1) * C], ALU.add)
                    for c in range(2):
                        P.tt(YA[:, :, q0 + c * C:q0 + (c + 1) * C], YSc[c].v, GG[:, :, c * C:(c + 1) * C], ALU.mult)
                    if True:
                        P.mark('rw_epi')
                    P.barrier()
            if l == 0 and ti == 0:
                dump("ya", YA.v)

            with contextlib.ExitStack() as ph:
                QT = P.sb(ph, [128, 4, TT], BF16, "QT")
                QR = [P.sb(ph, [128, TT], F32, "QR") for _ in range(2)]
                SQ = [P.sb(ph, [128, TT], BF16, "SQs") for _ in range(2)]
                RS = [P.sb(ph, [128, TT], F32, "RSs") for _ in range(2)]
                for hp2 in range(2):
                    wqk = [wload(w1024(w_in_d, l, 1792 + which * 512 + hp2 * 256, 256), 8, 256) for which in range(2)]
                    for pp in range(2):
                        p = hp2 * 2 + pp
                        W2 = range(2)
                        pqs, sss = [], []
                        for which in W2:
                            pq = nps()
                            pqs.append(pq)
                            for kc in range(8):
                                P.mm(pq.v, wqk[which][:, kc, pp * 128:(pp + 1) * 128], HT[:, kc, :], start=(kc == 0), stop=(kc == 7))
                        for which in W2:
                            P.cp(QR[which].v, pqs[which].v, eng="act")
                        for which in W2:
                            P.act(SQ[which].v, QR[which].v, AF.Square)
                        for which in W2:
                            ss = nps()
                            sss.append(ss)
                            P.mm(ss.v, bones_b, SQ[which].v)
                        P.act(RS[0].v, sss[0].v, AF.Ln, bias=EPS64, scale=1.0)
                        P.act(RS[1].v, sss[1].v, AF.Ln, bias=EPS6, scale=1.0 / 64)
                        for which in W2:
                            P.act(RS[which].v, RS[which].v, AF.Exp, scale=-0.5)
                        for which in W2:
                            dst = QT[:, p, :] if which == 0 else KT[:, p, t0:t0 + TT]
                            P.stt(dst, QR[which].v, PV[:, pv0 + 115 + which:pv0 + 116 + which], RS[which].v, ALU.mult, ALU.mult)
                for vp in range(2):
                    wt = wload(w1024(w_in_d, l, 1792 + 1024 + vp * 256, 256), 8, 256)
                    for tb in range(4):
                        pvv = nps()
                        for kc in range(8):
                            P.mm(pvv[:, 0:256], HT[:, kc, tb * 128:(tb + 1) * 128], wt[:, kc, :], start=(kc == 0), stop=(kc == 7))
                        P.cp(VC[:, ti * 4 + tb, vp * 256:(vp + 1) * 256], pvv[:, 0:256], eng=("act" if tb % 2 else "dve"))
                G = 4
                EZ = [P.sb(ph, [128, TT], F32, "EZ") for _ in range(G)]
                SP = [[P.sb(ph, [128, TT], BF16, "SP") for _ in range(G)] for _ in range(2)]
                ATb = [P.sb(ph, [128, TT], BF16, "ATb") for _ in range(G)]
                RRW = [P.sb(ph, [128, TT], F32, "RRW") for _ in range(G)]
                RHI = [P.sb(ph, [128, TT], BF16, "RHI") for _ in range(G)]
                nb = (t0 + TT) // 128
                zc = [0, 0]

                def bank(kind):
                    i = zc[kind] % 2
                    zc[kind] += 1
                    return PS[kind * 2 + i]

                def geom(kb):
                    bl = kb - ti * 4
                    cs = max(0, bl) * 128
                    return bl, cs, TT - cs

                def group(ps, mms):
                    n = len(mms)
                    for j, (o, a_, b_) in enumerate(mms):
                        P.mm(o, a_, b_, start=(j == 0), stop=(j == n - 1))

                def stage_a(grp, kb, slot):
                    bl, cs, ncol = geom(kb)
                    for i in range(G):
                        h = grp * G + i
                        p, b = h // 2, (h % 2) * 64
                        pz = bank(0)
                        mms = [(pz[:, 0:ncol], KT[b:b + 64, p, kb * 128:(kb + 1) * 128], QT[b:b + 64, p, cs:TT])]
                        if bl >= 0:
                            mms.append((pz[:, 0:128], ident_b, negm_b))
                        group(pz, mms)
                        P.act(EZ[i][:, 0:ncol], pz[:, 0:ncol], AF.Exp)
                    for i in range(G):
                        P.act(SP[slot][i][:, 0:ncol], EZ[i][:, 0:ncol], AF.Ln, bias=ONE)

                def stage_b(grp, kb, slot):
                    bl, cs, ncol = geom(kb)
                    first = (kb == nb - 1)
                    for i in range(G):
                        h = grp * G + i
                        p, b = h // 2, (h % 2) * 64
                        pzb = bank(1)
                        mms = [(pzb[:, 0:ncol], KT[b:b + 64, p, kb * 128:(kb + 1) * 128], QT[b:b + 64, p, cs:TT]),
                               (pzb[:, 0:ncol], ntri_b, SP[slot][i][:, 0:ncol])]
                        if not first:
                            mms.append((pzb[:, 0:ncol], nones_b, RHI[i][:, cs:TT]))
                        if bl >= 0:
                            mms.append((pzb[:, 0:128], ident_b, negm_b))
                        group(pzb, mms)
                        P.act(ATb[i][:, cs:TT], pzb[:, 0:ncol], AF.Exp)
                    if kb > 0:
                        for i in range(G):
                            P.tt(RRW[i][:, cs:TT], RRW[i][:, cs:TT], SP[slot][i][:, 0:ncol], ALU.add)
                            P.cp(RHI[i].v, RRW[i].v, eng=("act" if i == 3 else "dve"))
                    for i in range(G):
                        h = grp * G + i
                        P.mm(PS[4 + i][0:64, :], VC[:, kb, h * 64:(h + 1) * 64], ATb[i].v, start=first, stop=(kb == 0))

                for grp in range(8 // G):
                    for i in range(G):
                        P.memset(RRW[i].v, 0.0)
                        P.memset(RHI[i].v, 0.0, eng="pool")
                        P.memset(ATb[i].v, 0.0, eng="pool")
                    kbs = list(range(nb - 1, -1, -1))
                    stage_a(grp, kbs[0], 0)
                    for k, kb in enumerate(kbs):
                        if k + 1 < len(kbs):
                            stage_a(grp, kbs[k + 1], (k + 1) % 2)
                        stage_b(grp, kb, k % 2)
                    for i in range(G):
                        h = grp * G + i
                        P.cp(YB[:, h, :], PS[4 + i][0:64, :], eng=("act" if i % 2 else "dve"))
                P.barrier()
            if l == 0 and ti == 0:
                dump("yb", YB.v)

            with contextlib.ExitStack() as ph:
                QR = [P.sb(ph, [128, TT], F32, "QRm") for _ in range(2)]
                SQ = [P.sb(ph, [128, TT], BF16, "SQm2") for _ in range(2)]
                RS = [P.sb(ph, [128, TT], F32, "RSm2") for _ in range(2)]
                QM = [P.sb(ph, [128, TT], BF16, "QM") for _ in range(2)]
                ES = [[P.sb(ph, [128, TT], BF16, "ES") for _ in range(2)] for _ in range(2)]
                for hp2 in range(2):
                    wt = wload(w1024(w_in_d, l, 3328 + hp2 * 256, 256), 8, 256)
                    R2 = range(2)
                    pqs = []
                    for pp in R2:
                        pq = nps()
                        pqs.append(pq)
                        for kc in range(8):
                            P.mm(pq.v, wt[:, kc, pp * 128:(pp + 1) * 128], HT[:, kc, :], start=(kc == 0), stop=(kc == 7))
                    for pp in R2:
                        P.cp(QR[pp].v, pqs[pp].v, eng="act")
                    for pp in R2:
                        P.act(SQ[pp].v, QR[pp].v, AF.Square)
                    sss = []
                    for pp in R2:
                        ss = nps()
                        sss.append(ss)
                        P.mm(ss.v, ones_b, SQ[pp].v)
                    for pp in R2:
                        P.act(RS[pp].v, sss[pp].v, AF.Ln, bias=EPS128, scale=1.0)
                    for pp in R2:
                        P.act(RS[pp].v, RS[pp].v, AF.Exp, scale=-0.5)
                    for pp in R2:
                        P.stt(QM[pp].v, QR[pp].v, PV[:, pv0 + 117:pv0 + 118], RS[pp].v, ALU.mult, ALU.mult)
                    for mb in range(2):
                        psl = []
                        for pp in R2:
                            h = hp2 * 2 + pp
                            pss = nps()
                            psl.append(pss)
                            P.mm(pss.v, MK[:, h, mb * 128:(mb + 1) * 128], QM[pp].v)
                        for pp in R2:
                            P.act(ES[pp][mb].v, psl[pp].v, AF.Exp)
                    pds, pos = [], []
                    for pp in R2:
                        h = hp2 * 2 + pp
                        pd = nps()
                        pds.append(pd)
                        P.mm(pd.v, ones_b, ES[pp][0].v, start=True, stop=False)
                        P.mm(pd.v, ones_b, ES[pp][1].v, start=False, stop=True)
                        po = nps()
                        pos.append(po)
                        P.mm(po.v, MV[:, 0, h * 128:(h + 1) * 128], ES[pp][0].v, start=True, stop=False)
                        P.mm(po.v, MV[:, 1, h * 128:(h + 1) * 128], ES[pp][1].v, start=False, stop=True)
                    for pp in R2:
                        P.act(RS[pp].v, pds[pp].v, AF.Ln)
                    for pp in R2:
                        P.act(RS[pp].v, RS[pp].v, AF.Exp, scale=-1.0)
                    for pp in R2:
                        P.tt(YM[:, hp2 * 2 + pp, :], pos[pp].v, RS[pp].v, ALU.mult)
                P.barrier()
            if l == 0 and ti == 0:
                dump("ym", YM.v)

            with contextlib.ExitStack() as ph:
                MG = P.sb(ph, [128, 8, TT], BF16, "MG")
                SGm = [P.sb(ph, [128, TT], F32, "SGm") for _ in range(3)]
                WB = [[P.sb(ph, [128, 8, 256], BF16, f"WB{i}") for i in range(3)] for _ in range(2)]
                AC = [[P.sb(ph, [128, TT], F32, "AC") for _ in range(3)] for _ in range(2)]
                for d2 in range(4):
                    wb = WB[d2 % 2]
                    for b in range(2):
                        P.dma("pool", wb[b][0:64, :, :], w_br_d[l][b].re("(h p) n -> p h n", p=64)[:, :, d2 * 256:(d2 + 1) * 256])
                    P.dma("pool", wb[2][:, 0:4, :], w_br_d[l][2].re("(c p) n -> p c n", p=128)[:, :, d2 * 256:(d2 + 1) * 256])
                    gw = [wload(w1024(w_in_d, l, 3840 + b * 1024 + d2 * 256, 256), 8, 256) for b in range(3)]
                    for dd in range(2):
                        d = d2 * 2 + dd
                        ac = AC[d % 2]
                        cols = slice(dd * 128, (dd + 1) * 128)
                        for b in range(3):
                            pg = nps()
                            for kc in range(8):
                                P.mm(pg.v, gw[b][:, kc, cols], HT[:, kc, :], start=(kc == 0), stop=(kc == 7))
                            P.act(SGm[b].v, pg.v, AF.Sigmoid)
                            pb = nps()
                            if b < 2:
                                ysrc = YA if b == 0 else YB
                                for h in range(8):
                                    P.mm(pb.v, wb[b][0:64, h, cols], ysrc[:, h, :], start=(h == 0), stop=(h == 7))
                            else:
                                for c4 in range(4):
                                    P.mm(pb.v, wb[2][:, c4, cols], YM[:, c4, :], start=(c4 == 0), stop=(c4 == 3))
                            P.tt(ac[b].v, SGm[b].v, pb.v, ALU.mult)
                        P.tt(ac[0].v, ac[0].v, ac[1].v, ALU.add)
                        P.tt(MG[:, d, :], ac[0].v, ac[2].v, ALU.add)
                if l == 0 and ti == 0:
                    dump("merged", MG.v)
                for d2 in range(4):
                    wt = wload(w1024(w_out_d, l, d2 * 256, 256), 8, 256)
                    for dd in range(2):
                        d = d2 * 2 + dd
                        px = nps()
                        for kc in range(8):
                            P.mm(px.v, wt[:, kc, dd * 128:(dd + 1) * 128], MG[:, kc, :], start=(kc == 0), stop=(kc == 7))
                        P.tt(XT[:, d, :], XT[:, d, :], px.v, ALU.add)
                P.barrier()
            if l == 0 and ti == 0:
                dump("xattn", XT.v)

            with contextlib.ExitStack() as ph:
                rmsnorm(ph, XT, 8, pv0 + 8, HT, TT)
                ACTB = P.sb(ph, [128, 22, TT], BF16, "ACTB")
                WD = [P.sb(ph, [128, 22, 128], BF16, f"WD{i}") for i in range(3)]
                HB = [P.sb(ph, [128, TT + 2], BF16, "HB") for _ in range(4)]
                DG = [P.sb(ph, [128, 3, 128], BF16, "DG") for _ in range(4)]
                ACg = [P.sb(ph, [128, TT], F32, "ACg") for _ in range(3)]
                ACv = [P.sb(ph, [128, TT], F32, "ACv") for _ in range(3)]
                specs = []
                for c2 in range(11):
                    for dd in range(2):
                        specs.append(("g", c2, dd))
                        specs.append(("v", c2, dd))
                wcur = {}
                stt_ = {}

                def ffn_up(k):
                    kind, c2, dd = specs[k]
                    c = c2 * 2 + dd
                    ci = c if kind == "g" else 22 + c
                    if dd == 0:
                        wcur[kind] = wload(w1024(w_up_d, l, (0 if kind == "g" else DFF) + c2 * 256, 256), 8, 256)
                    wt = wcur[kind]
                    pu = nps()
                    for kc in range(8):
                        P.mm(pu.v, wt[:, kc, dd * 128:(dd + 1) * 128], HT[:, kc, :], start=(kc == 0), stop=(kc == 7))
                    hb = HB[k % 4]
                    dg = DG[k % 4]
                    for j in range(3):
                        wcol = PV[:, pv0 + 163 + 44 * j + ci:pv0 + 164 + 44 * j + ci]
                        P.act(dg[:, j, :], ident, AF.Identity, scale=wcol)
                    P.cp(hb[:, 0:2], HALO[:, ci, :])
                    P.cp(hb[:, 2:TT + 2], pu.v, eng="act")
                    P.cp(HALO[:, ci, :], hb[:, TT:TT + 2])
                    stt_[k] = (hb, dg, ci, c, kind)

                def ffn_conv(k):
                    hb, dg, ci, c, kind = stt_.pop(k)
                    acc = (ACg if kind == "g" else ACv)[c % 3]
                    pc = nps()
                    for j in range(3):
                        P.mm(pc.v, dg[:, j, :], hb[:, j:j + TT], start=(j == 0), stop=(j == 2))
                    P.act(acc.v, pc.v, AF.Silu if kind == "g" else AF.Identity, bias=PV[:, pv0 + 119 + ci:pv0 + 120 + ci])
                    if kind == "v":
                        P.tt(ACTB[:, c, :], ACg[c % 3].v, ACv[c % 3].v, ALU.mult)

                ffn_up(0)
                for k in range(len(specs)):
                    if k + 1 < len(specs):
                        ffn_up(k + 1)
                    ffn_conv(k)
                for d in range(8):
                    wd = WD[st["wd"] % 3]
                    st["wd"] += 1
                    src = w_dn_d[l][d].re("p (c n) -> p c n", n=128)
                    for a, bnd in ((0, 11), (11, 22)):
                        P.dma("pool", wd[:, a:bnd, :], src[:, a:bnd, :])
                    px = nps()
                    for c in range(22):
                        P.mm(px.v, wd[:, c, :], ACTB[:, c, :], start=(c == 0), stop=(c == 21))
                    P.tt(XT[:, d, :], XT[:, d, :], px.v, ALU.add)
                for a in range(0, 8, 4):
                    P.dma("sp", xdst.v.re("(c p) n -> p c n", p=128)[:, a:a + 4, t0:t0 + TT], XT[:, a:a + 4, :],
                          semtile=XT, is_output=last)
                P.barrier()
    P.emit()
    root.close()
    nc._marks = P.marks
    return nc


def _consts():
    c = np.zeros((128, NCONST), np.float32)
    c[:, 0:128] = np.eye(128)
    c[:, 128:256] = 1.0
    bo = np.zeros((128, 128), np.float32)
    bo[0:64, 0:64] = 1.0
    bo[64:128, 64:128] = 1.0
    c[:, 256:384] = bo
    s = np.arange(64)[:, None]
    t = np.arange(64)[None, :]
    c[0:64, 384:448] = (s < t)
    c[0:64, 448:512] = (s <= t)
    c[0:64, 512:576] = (s > t)
    c[0:64, 576:640] = (s == t)
    s = np.arange(128)[:, None]
    t = np.arange(128)[None, :]
    c[:, 640:768] = (s < t)
    rm = np.ones((64, RT), np.float32)
    rm[:, 0::C] = 0.0
    c[0:64, 768:768 + RT] = rm
    c[:, 896:1024] = (s >= t)
    c[:, 1024] = 1e-6
    c[:, 1025] = 64e-6
    c[:, 1026] = 128e-6
    c[:, 1027] = 64e-5
    c[:, 1028] = 1.0
    c[:, 1029] = 1e-18
    c[0, 1152:1280] = 1.0
    c[:, 1280:1408] = -(s >= t).astype(np.float32)
    c[:, 1408:1536] = -1.0
    c[:, 1536:1664] = -30000.0 * (s >= t)
    c[0:64, 1664:1728] = np.eye(64) - 1.0 / 64
    return c


def _pack(inp):
    pv = np.zeros((128, DEPTH * PVS), np.float32)
    sm = np.zeros((DEPTH, 128, SMW), np.float32)
    for l in range(DEPTH):
        o = l * PVS

        def fm(v, n):
            return np.asarray(v, np.float32).reshape(n, 128).T

        def hm(v):
            return np.asarray(v, np.float32).reshape(8, 64).T

        pv[:, o + 0:o + 8] = fm(inp["norm1_g"][l], 8)
        pv[:, o + 8:o + 16] = fm(inp["norm2_g"][l], 8)
        pv[:, o + 16:o + 24] = fm(inp["mem_norm_g"][l], 8)
        mu = np.asarray(inp["shift_mu"][l], np.float32)
        pv[0:64, o + 24:o + 32] = hm(mu[0:512])
        pv[0:64, o + 32:o + 40] = hm(mu[512:1024])
        pv[0:64, o + 40:o + 48] = hm(mu[1024:1536])
        pv[0:64, o + 48] = mu[1536:1600]
        pv[0:64, o + 49] = mu[1600:1664]
        pv[:, o + 50] = mu[1664:1792]
        pv[0:64, o + 51:o + 59] = hm(inp["decay_w0"][l])
        pv[0:64, o + 59:o + 67] = hm(inp["iclr_a0"][l])
        pv[0:64, o + 67:o + 75] = hm(inp["k_k"][l])
        pv[0:64, o + 75:o + 83] = hm(inp["k_a"][l])
        pv[0:64, o + 83:o + 91] = hm(np.asarray(inp["r_k"][l]).reshape(-1))
        pv[0:64, o + 91:o + 99] = hm(inp["lnx_g"][l])
        pv[0:64, o + 99:o + 107] = hm(inp["lnx_b"][l])
        if l > 0:
            pv[0:64, o + 107:o + 115] = hm(inp["vres_v0"][l - 1])
        pv[:, o + 115] = np.tile(np.asarray(inp["sb_q_norm_g"][l], np.float32), 2)
        pv[:, o + 116] = np.tile(np.asarray(inp["sb_k_norm_g"][l], np.float32), 2)
        pv[:, o + 117] = inp["mem_q_norm_g"][l]
        pv[:, o + 118] = inp["mem_k_norm_g"][l]
        pv[:, o + 119:o + 163] = fm(inp["conv_b"][l], 44)
        cw = np.asarray(inp["conv_w"][l], np.float32)
        pv[:, o + 163:o + 207] = fm(cw[0], 44)
        pv[:, o + 207:o + 251] = fm(cw[1], 44)
        pv[:, o + 251:o + 295] = fm(cw[2], 44)
        sm[l, 0:64, 0:512] = inp["decay_w2"][l]
        sm[l, 0:64, 512:1024] = inp["iclr_a2"][l]
        sm[l, :, 1024:1536] = inp["gate_g2"][l]
        if l > 0:
            v1 = np.asarray(inp["vres_v1"][l - 1], np.float32)
            sm[l, 0:64, 1536:1536 + 256] = v1.reshape(8, 64, 32).transpose(1, 0, 2).reshape(64, 256)
    return pv, sm


def _pack2(inp, sm):
    for l in range(1, DEPTH):
        sm[l, 0:32, 1792:1792 + 512] = inp["vres_v2"][l - 1]
    return sm


def _wdown(w):
    w = np.asarray(w, np.float32).reshape(DEPTH, 22, 128, 8, 128).transpose(0, 3, 2, 1, 4)
    return np.ascontiguousarray(w).reshape(DEPTH, 8, 128, 22 * 128)


_CACHE = {}


def kernel(**inputs):
    inp = {k: np.asarray(v) for k, v in inputs.items()}
    x = inp["x"].astype(np.float32, copy=False)
    mem = inp["mem"].astype(np.float32, copy=False)
    pv, sm = _pack(inp)
    sm = _pack2(inp, sm)
    consts = _consts()
    if "nc" not in _CACHE:
        _CACHE["nc"] = build()
    nc = _CACHE["nc"]
    shared = {
        "pvec": pv, "smat": sm, "consts": consts,
        "w_in": np.ascontiguousarray(inp["w_in"], np.float32),
        "w_mem_kv": np.ascontiguousarray(inp["w_mem_kv"], np.float32),
        "w_branch": np.ascontiguousarray(inp["w_branch"], np.float32),
        "w_out": np.ascontiguousarray(inp["w_out"], np.float32),
        "w_up": np.ascontiguousarray(inp["w_up"], np.float32),
        "w_down": _wdown(inp["w_down"]),
    }
    in_maps = []
    for b in range(8):
        m = dict(shared)
        m["xT"] = np.ascontiguousarray(x[b].T)
        m["memT"] = np.ascontiguousarray(mem[b].T)
        in_maps.append(m)
    res = run_bass_kernel_spmd(nc, in_maps, core_ids=list(range(8)))
    out = np.stack([np.ascontiguousarray(res.results[b]["yT"].T) for b in range(8)], axis=0)
    return out.astype(np.float32)
```

```python
import math
import os
import contextlib
import numpy as np
import concourse.bass as bass
import concourse.mybir as mybir
from concourse.bass_utils import run_bass_kernel_spmd

F32 = mybir.dt.float32
F32R = mybir.dt.float32r
BF16 = mybir.dt.bfloat16
AF = mybir.ActivationFunctionType
ALU = mybir.AluOpType

D = 1024
T = 2048
TT = 512
NT = T // TT
RT = 128
C = 64
DFF = 2816
INC = 6912
PVS = 320
NCONST = 1728
SMW = 2304
DEPTH = 2
C0 = math.exp(-0.5)

DEBUG = {}


class Res:
    __slots__ = ("name", "last_w", "readers")

    def __init__(self, name):
        self.name = name
        self.last_w = None
        self.readers = {}


class V:
    __slots__ = ("t", "ap")

    def __init__(self, t, ap):
        self.t = t
        self.ap = ap

    def __getitem__(self, k):
        return V(self.t, self.ap[k])

    def bc(self, shape):
        return V(self.t, self.ap.to_broadcast(list(shape)))

    def cast(self, dt):
        return V(self.t, self.ap.bitcast(dt))

    def re(self, pat, **kw):
        return V(self.t, self.ap.rearrange(pat, **kw))


class Tl(Res):
    def __init__(self, h, name, ap=None):
        Res.__init__(self, name)
        self.h = h
        self._ap = ap
        self.dma_sem = None
        self.dma_cnt = 0

    def __getitem__(self, k):
        base = self._ap if self._ap is not None else self.h
        return V(self, base[k])

    @property
    def v(self):
        return self[:]


ENGS = ["pe", "dve", "act", "pool", "sp"]


class Prog:
    def __init__(self, nc):
        self.nc = nc
        self.q = {e: [] for e in ENGS}
        self.sem = {e: nc.alloc_semaphore(f"prog_{e}") for e in ENGS}
        self.cnt = {e: 0 for e in ENGS}
        self.known = {e: {} for e in ENGS}
        self.n = 0
        self.out_tokens = []
        self.dma_tokens = {}
        self.marks = []
        self.pending_pool = None
        self.ghost = {}
        self.sem_pool = []
        self.dma_tiles = []

    def sb(self, stack, shape, dtype=F32, name="t"):
        self.n += 1
        h = stack.enter_context(self.nc.sbuf_tensor(f"{name}_{self.n}", list(shape), dtype))
        t = Tl(h, f"{name}_{self.n}")
        t.persist = getattr(stack, "is_root", False)
        t.readers = dict(self.ghost)
        return t

    def ps(self, stack, shape, dtype=F32, name="p"):
        self.n += 1
        h = stack.enter_context(self.nc.psum_tensor(f"{name}_{self.n}", list(shape), dtype))
        return Tl(h, f"{name}_{self.n}")

    def _collect(self, eng, reads, writes, sync_same):
        deps = {}

        def add(tok):
            if tok is None:
                return
            s, v = tok
            if deps.get(s, 0) < v:
                deps[s] = v

        for t in reads:
            add(t.last_w)
        for t in writes:
            add(t.last_w)
            for s, v in t.readers.items():
                add((s, v))
        waits = []
        kn = self.known[eng]
        own = self.sem[eng]
        for s, v in deps.items():
            if s is own and not sync_same:
                continue
            if kn.get(s, 0) >= v:
                continue
            kn[s] = v
            waits.append((s, v))
        return waits

    def _commit(self, tok, reads, writes):
        s, v = tok
        for t in reads:
            if t.readers.get(s, 0) < v:
                t.readers[s] = v
        for t in writes:
            t.last_w = tok
            t.readers = {}

    def op(self, eng, fn, reads=(), writes=(), sync_same=True):
        if eng == "pool":
            self._flush_pool(list(reads) + list(writes))
        waits = self._collect(eng, reads, writes, sync_same)
        self.cnt[eng] += 1
        tok = (self.sem[eng], self.cnt[eng])
        self.q[eng].append((waits, fn, tok, 1))
        self._commit(tok, reads, writes)
        return tok

    def dma(self, q, out, in_, semtile=None, is_output=False):
        reads = [in_.t]
        writes = [out.t]
        if q == "pool":
            self._flush_pool(reads + writes)
        waits = self._collect(q, reads, writes, True)
        if semtile is None:
            semtile = out.t if isinstance(out.t, Tl) and out.t.h is not None and not getattr(out.t, "is_dram", False) else in_.t
        if semtile.dma_sem is None:
            if self.sem_pool:
                semtile.dma_sem, semtile.dma_cnt = self.sem_pool.pop()
            else:
                semtile.dma_sem = self.nc.alloc_semaphore(f"dma_{semtile.name}")
            self.dma_tiles.append(semtile)
        semtile.dma_cnt += 16
        tok = (semtile.dma_sem, semtile.dma_cnt)
        oa, ia = out.ap, in_.ap
        self.q[q].append((waits, lambda e: e.dma_start(out=oa, in_=ia), tok, 16))
        self._commit(tok, reads, writes)
        self.dma_tokens[semtile.dma_sem] = tok
        if is_output:
            self.out_tokens.append(tok)
        return tok

    def _flush_pool(self, res):
        if self.pending_pool is None or all(getattr(t, "persist", False) for t in res):
            return
        waits = []
        kn = self.known["pool"]
        for s, v in self.pending_pool:
            if kn.get(s, 0) < v:
                kn[s] = v
                waits.append((s, v))
        if waits:
            self.q["pool"].append((waits, None, None, 0))
        self.pending_pool = None
        self.ghost = {}

    def mark(self, name):
        self.marks.append((name, dict(self.cnt)))

    def barrier(self):
        toks = [(self.sem[e], self.cnt[e]) for e in ENGS if self.cnt[e] > 0]
        toks += list(self.dma_tokens.values())
        g = {}
        for s_, v in toks:
            if g.get(s_, 0) < v:
                g[s_] = v
        self.ghost = g
        keep = []
        for t in self.dma_tiles:
            if getattr(t, "persist", False):
                keep.append(t)
            else:
                self.sem_pool.append((t.dma_sem, t.dma_cnt))
                t.dma_sem = None
        self.dma_tiles = keep

    def mm(self, out, lhsT, rhs, start=True, stop=True):
        oa, la, ra = out.ap, lhsT.ap, rhs.ap
        self.op("pe", lambda e: e.matmul(oa, lhsT=la, rhs=ra, start=start, stop=stop),
                reads=[lhsT.t, rhs.t], writes=[out.t], sync_same=False)

    def tr(self, out, in_, ident):
        oa, ia, da = out.ap, in_.ap, ident.ap
        self.op("pe", lambda e: e.transpose(oa, ia, da), reads=[in_.t, ident.t], writes=[out.t], sync_same=False)

    def act(self, out, in_, func, bias=None, scale=None):
        oa, ia = out.ap, in_.ap
        reads = [in_.t]
        kw = {}
        if bias is not None:
            if isinstance(bias, V):
                kw["bias"] = bias.ap
                reads.append(bias.t)
            else:
                kw["bias"] = bias
        if scale is not None:
            if isinstance(scale, V):
                kw["scale"] = scale.ap
                reads.append(scale.t)
            else:
                kw["scale"] = scale
        self.op("act", lambda e: e.activation(out=oa, in_=ia, func=func, **kw), reads=reads, writes=[out.t])

    def tt(self, out, in0, in1, op, eng="dve"):
        oa, a, b = out.ap, in0.ap, in1.ap
        self.op(eng, lambda e: e.tensor_tensor(out=oa, in0=a, in1=b, op=op), reads=[in0.t, in1.t], writes=[out.t])

    def ts(self, out, in0, s1, op0, s2=None, op1=None, eng="dve"):
        oa, a = out.ap, in0.ap
        reads = [in0.t]
        if isinstance(s1, V):
            reads.append(s1.t)
            s1 = s1.ap
        if isinstance(s2, V):
            reads.append(s2.t)
            s2 = s2.ap
        if op1 is None:
            self.op(eng, lambda e: e.tensor_scalar(out=oa, in0=a, scalar1=s1, scalar2=None, op0=op0), reads=reads, writes=[out.t])
        else:
            self.op(eng, lambda e: e.tensor_scalar(out=oa, in0=a, scalar1=s1, scalar2=s2, op0=op0, op1=op1), reads=reads, writes=[out.t])

    def stt(self, out, in0, scalar, in1, op0, op1):
        oa, a, b = out.ap, in0.ap, in1.ap
        reads = [in0.t, in1.t]
        if isinstance(scalar, V):
            reads.append(scalar.t)
            scalar = scalar.ap
        self.op("dve", lambda e: e.scalar_tensor_tensor(out=oa, in0=a, scalar=scalar, in1=b, op0=op0, op1=op1),
                reads=reads, writes=[out.t])

    def cp(self, out, in_, eng="dve"):
        oa, ia = out.ap, in_.ap
        if eng == "act":
            self.op("act", lambda e: e.copy(out=oa, in_=ia), reads=[in_.t], writes=[out.t])
        else:
            self.op(eng, lambda e: e.tensor_copy(out=oa, in_=ia), reads=[in_.t], writes=[out.t])

    def recip(self, out, in_):
        oa, ia = out.ap, in_.ap
        self.op("dve", lambda e: e.reciprocal(out=oa, in_=ia), reads=[in_.t], writes=[out.t])

    def scan(self, out, d0, d1, init, op0, op1):
        oa, a, b = out.ap, d0.ap, d1.ap
        self.op("dve", lambda e: e.tensor_tensor_scan(out=oa, data0=a, data1=b, initial=init, op0=op0, op1=op1),
                reads=[d0.t, d1.t], writes=[out.t])

    def memset(self, out, val, eng="dve"):
        oa = out.ap
        self.op(eng, lambda e: e.memset(oa, val), reads=[], writes=[out.t])

    def emit(self):
        nc = self.nc
        final = {}
        for s, v in self.out_tokens:
            if final.get(s, 0) < v:
                final[s] = v
        final_waits = list(final.items())
        q = self.q

        def run(e, lst, extra=()):
            for waits, fn, tok, inc in lst:
                for s, v in waits:
                    e.wait_ge(s, v)
                if fn is not None:
                    fn(e).then_inc(tok[0], inc)
            for s, v in extra:
                e.wait_ge(s, v)

        with nc.Block() as block:
            @block.tensor
            def _(e):
                run(e, q["pe"])

            @block.vector
            def _(e):
                run(e, q["dve"])

            @block.scalar
            def _(e):
                run(e, q["act"])

            @block.gpsimd
            def _(e):
                run(e, q["pool"])

            @block.sync
            def _(e):
                run(e, q["sp"], final_waits)


def build(n_layers=DEPTH, n_tiles=NT, dbg=None):
    nc = bass.Bass("TRN2", target_bir_lowering=False)
    P = Prog(nc)

    def dram(name, shape, kind, dt=F32):
        h = nc.dram_tensor(name, list(shape), dt, kind=kind)
        t = Tl(None, name, ap=h.ap())
        t.is_dram = True
        t.persist = True
        return t

    xT_d = dram("xT", [D, T], "ExternalInput")
    memT_d = dram("memT", [D, 256], "ExternalInput")
    pvec_d = dram("pvec", [128, DEPTH * PVS], "ExternalInput")
    smat_d = dram("smat", [DEPTH, 128, SMW], "ExternalInput")
    const_d = dram("consts", [128, NCONST], "ExternalInput")
    w_in_d = dram("w_in", [DEPTH, INC // 256, 128, 2048], "ExternalInput")
    w_kv_d = dram("w_mem_kv", [DEPTH, 4, 128, 2048], "ExternalInput")
    w_br01_d = dram("w_br01", [DEPTH, 2, 4, 64, 2048], "ExternalInput")
    w_br2_d = dram("w_br2", [DEPTH, 4, 128, 1024], "ExternalInput")
    w_out_d = dram("w_out", [DEPTH, 4, 128, 2048], "ExternalInput")
    w_up_d = dram("w_up", [DEPTH, 2 * DFF // 256, 128, 2048], "ExternalInput")
    w_dn_d = dram("w_down", [DEPTH, 8, 128, 22 * 128], "ExternalInput")
    yT_d = dram("yT", [D, T], "ExternalOutput")
    xs_d = dram("xs", [D, T], "Internal")
    vf_d = dram("vf", [8, 64, T], "Internal")
    dbg_d = {}
    if dbg:
        for k, shp in dbg.items():
            dbg_d[k] = dram("dbg_" + k, shp, "ExternalOutput")

    root = contextlib.ExitStack()
    root.is_root = True
    CT = P.sb(root, [128, NCONST], F32, "CT")
    CB = P.sb(root, [128, 1024], BF16, "CB")
    RM8 = P.sb(root, [64, 8, RT], BF16, "RM8")
    PV = P.sb(root, [128, DEPTH * PVS], F32, "PV")
    OMM = P.sb(root, [128, DEPTH * 32], F32, "OMM")
    SM = P.sb(root, [128, SMW], F32, "SM")
    SMb = P.sb(root, [128, 1536], BF16, "SMb")
    KT = P.sb(root, [128, 4, T], BF16, "KT")
    VC = P.sb(root, [128, T // 128, 512], BF16, "VC")
    MK = P.sb(root, [128, 4, 256], BF16, "MK")
    MV = P.sb(root, [128, 2, 512], BF16, "MV")
    XT = P.sb(root, [128, 8, TT], F32, "XT")
    HT = P.sb(root, [128, 8, TT], BF16, "HT")
    NWB = 6
    WT = [P.sb(root, [128, 8, 256], BF16, f"WT{i}") for i in range(NWB)]
    Hst = P.sb(root, [64, 8, 64], F32, "Hst")
    Hb = P.sb(root, [64, 8, 64], BF16, "Hb")
    PREV = P.sb(root, [128, 32], F32, "PREV")
    HALO = P.sb(root, [128, 44, 2], F32, "HALO")
    YA = P.sb(root, [64, 8, TT], BF16, "YA")
    YB = P.sb(root, [64, 8, TT], BF16, "YB")
    YM = P.sb(root, [128, 4, TT], BF16, "YM")
    PS = [P.ps(root, [128, 512], F32, f"PS{i}") for i in range(8)]
    st = {"ps": 0, "w": 0, "wd": 0}

    def nps(k=6):
        i = st["ps"] % k
        st["ps"] += 1
        return PS[i]

    ident = CT[:, 0:128]
    ones = CT[:, 128:256]
    bones = CT[:, 256:384]
    su = CT[0:64, 384:448]
    iu = CT[0:64, 448:512]
    sl = CT[0:64, 512:576]
    id64 = CT[0:64, 576:640]
    smk = CT[:, 640:768]
    rmask = CT[0:64, 768:896]
    tri = CT[:, 896:1024]
    EPS6 = CT[:, 1024:1025]
    EPS64 = CT[:, 1025:1026]
    EPS128 = CT[:, 1026:1027]
    EPSLN = CT[:, 1027:1028]
    ONE = CT[:, 1028:1029]
    EPS18 = CT[:, 1029:1030]
    tri_b = CB[:, 0:128]
    ones_b = CB[:, 128:256]
    Cm = CT[0:64, 1664:1728]
    ident_b = CB[:, 384:512]
    bones_b = CB[:, 896:1024]
    rmask8 = RM8.v.re("p h t -> p (h t)")
    ntri_b = CB[:, 512:640]
    nones_b = CB[:, 640:768]
    negm_b = CB[:, 768:896]
    sel0_b = CB[:, 256:384]

    def m8(mv):
        return V(mv.t, mv.ap.rearrange("p (o c) -> p o c", o=1).to_broadcast([64, 8, 64]))

    P.dma("sp", CT.v, const_d.v)
    P.dma("sp", PV.v, pvec_d.v)
    P.cp(CB[:, 0:128], tri)
    P.cp(CB[:, 128:256], ones)
    P.cp(CB[:, 256:384], CT[:, 1152:1280])
    P.cp(CB[:, 384:512], ident)
    P.cp(CB[:, 512:896], CT[:, 1280:1664])
    P.cp(CB[:, 896:1024], bones)
    P.cp(RM8.v, rmask.re("p (o t) -> p o t", o=1).bc([64, 8, RT]))
    for l in range(DEPTH):
        P.ts(OMM[:, l * 32:l * 32 + 27], PV[:, l * PVS + 24:l * PVS + 51], -1.0, ALU.mult, 1.0, ALU.add)

    def wload(src_rows_view, kc, ncols):
        wt = WT[st["w"] % NWB]
        st["w"] += 1
        half = kc
        for a in range(0, kc, half):
            b = min(kc, a + half)
            P.dma("pool", wt[:, a:b, 0:ncols], src_rows_view[:, a:b, :])
        return wt

    def w1024(wd, l, c0, ncols):
        assert c0 % 256 == 0 and ncols <= 256
        return wd[l][c0 // 256].re("p (c n) -> p c n", n=256)[:, :, 0:ncols]

    def rmsnorm(phase, src, n, gcol0, dst, N):
        SQ = [P.sb(phase, [128, N], BF16, "SQ") for _ in range(4)]
        RS = P.sb(phase, [128, N], F32, "RS")
        ss = nps()
        for c in range(n):
            s = SQ[c % 4]
            P.act(s.v, src[:, c, :], AF.Square)
            P.mm(ss[:, 0:N], ones_b, s.v, start=(c == 0), stop=(c == n - 1))
        P.act(RS.v, ss[:, 0:N], AF.Ln, bias=EPS6, scale=1.0 / (128 * n))
        P.act(RS.v, RS.v, AF.Exp, scale=-0.5)
        for c in range(n):
            P.stt(dst[:, c, :], src[:, c, :], PV[:, gcol0 + c:gcol0 + c + 1], RS.v, ALU.mult, ALU.mult)

    def dump(name, view):
        if dbg and name in dbg_d:
            P.dma("pool", dbg_d[name].v, view)

    for l in range(n_layers):
        pv0 = l * PVS
        last = (l == n_layers - 1)
        xsrc = xT_d if l == 0 else xs_d
        xdst = yT_d if last else xs_d
        P.dma("sp", SM.v, smat_d[l])
        P.cp(SMb.v, SM[:, 0:1536])
        P.memset(Hst.v, 0.0)
        P.memset(Hb.v, 0.0)
        P.memset(PREV.v, 0.0)
        P.memset(HALO.v, 0.0)

        with contextlib.ExitStack() as ph:
            MT = P.sb(ph, [128, 8, 256], F32, "MT")
            MN = P.sb(ph, [128, 8, 256], BF16, "MN")
            KR = P.sb(ph, [128, 256], F32, "KR")
            SQ = P.sb(ph, [128, 256], F32, "SQm")
            RS = P.sb(ph, [128, 256], F32, "RSm")
            P.dma("sp", MT.v, memT_d.v.re("(c p) n -> p c n", p=128))
            rmsnorm(ph, MT, 8, pv0 + 16, MN, 256)
            for hp in range(2):
                wt = wload(w1024(w_kv_d, l, hp * 256, 256), 8, 256)
                for hh in range(2):
                    h = hp * 2 + hh
                    pk = nps()
                    for kc in range(8):
                        P.mm(pk[:, 0:256], wt[:, kc, hh * 128:(hh + 1) * 128], MN[:, kc, :], start=(kc == 0), stop=(kc == 7))
                    P.cp(KR.v, pk[:, 0:256], eng="act")
                    P.act(SQ.v, KR.v, AF.Square)
                    ss = nps()
                    P.mm(ss[:, 0:256], ones, SQ.v)
                    P.act(RS.v, ss[:, 0:256], AF.Ln, bias=EPS6, scale=1.0 / 128)
                    P.act(RS.v, RS.v, AF.Exp, scale=-0.5)
                    P.stt(MK[:, h, :], KR.v, PV[:, pv0 + 118:pv0 + 119], RS.v, ALU.mult, ALU.mult)
            for vp in range(2):
                wt = wload(w1024(w_kv_d, l, 512 + vp * 256, 256), 8, 256)
                for mb in range(2):
                    pvv = nps()
                    for kc in range(8):
                        P.mm(pvv[:, 0:256], MN[:, kc, mb * 128:(mb + 1) * 128], wt[:, kc, :], start=(kc == 0), stop=(kc == 7))
                    P.cp(MV[:, mb, vp * 256:(vp + 1) * 256], pvv[:, 0:256], eng="act")
            P.barrier()

        for ti in range(n_tiles):
            t0 = ti * TT
            with contextlib.ExitStack() as ph:
                for a in range(0, 8, 4):
                    P.dma("sp", XT[:, a:a + 4, :], xsrc.v.re("(c p) n -> p c n", p=128)[:, a:a + 4, t0:t0 + TT])
                rmsnorm(ph, XT, 8, pv0 + 0, HT, TT)
                P.barrier()
            if l == 0 and ti == 0:
                dump("h", HT.v)

            for sub in range(TT // RT):
                q0 = sub * RT
                g0 = t0 + q0
                with contextlib.ExitStack() as ph:
                    GG = P.sb(ph, [64, 8, RT], F32, "GG")
                    EV = P.sb(ph, [64, 8, RT], F32, "EV")
                    GC = P.sb(ph, [64, 8, 2], F32, "GC")
                    RTb, ATb, BTb, KTb, VVb = [P.sb(ph, [64, 8, RT], BF16, n) for n in ("RTb", "ATb", "BTb", "KTb", "VVb")]
                    VTK = P.sb(ph, [64, 2, 512], BF16, "VTK")
                    BTK = P.sb(ph, [64, 2, 512], BF16, "BTK")
                    KTK = P.sb(ph, [64, 2, 512], BF16, "KTK")
                    ph2 = contextlib.ExitStack()

                    def fb(nm):
                        return P.sb(ph2, [64, 8, RT], F32, nm)
                    E1, E2, E3, AA, SG, CU, RR, VV = [fb(n) for n in ("E1", "E2", "E3", "AA", "SG", "CU", "RR", "VV")]
                    PR = [P.sb(ph2, [64, 4, RT + 1], F32, "PR")] * 2
                    TM = [P.sb(ph2, [64, 4, RT], F32, "TM")] * 2
                    WL = P.sb(ph2, [64, RT], BF16, "WL")
                    AL = P.sb(ph2, [64, RT], BF16, "AL")
                    GL = P.sb(ph2, [128, RT + 1], F32, "GL")
                    GS = P.sb(ph2, [128, RT], F32, "GS")
                    GSb = P.sb(ph2, [128, RT], BF16, "GSb")
                    prc = [0]

                    def shift4(psv, nh, prev0, mu0, dst):
                        pom = PR[prc[0] % 2]
                        pmu = TM[prc[0] % 2]
                        prc[0] += 1
                        prevv = PREV[0:64, prev0:prev0 + nh].re("p (h o) -> p h o", o=1)
                        muv = PV[0:64, pv0 + mu0:pv0 + mu0 + nh].re("p (h o) -> p h o", o=1)
                        for hh in range(nh):
                            mu1 = PV[0:64, pv0 + mu0 + hh:pv0 + mu0 + hh + 1]
                            om1 = OMM[0:64, l * 32 + (mu0 - 24) + hh:l * 32 + (mu0 - 24) + hh + 1]
                            P.act(pom[:, hh, 0:RT], psv[:, hh, :], AF.Identity, scale=om1)
                            P.act(pmu[:, hh, 1:RT], psv[:, hh, 0:RT - 1], AF.Identity, scale=mu1)
                        P.tt(pmu[:, 0:nh, 0:1], prevv, muv, ALU.mult)
                        P.cp(prevv, psv[:, :, RT - 1:RT], eng="act")
                        P.tt(dst, pom[:, 0:nh, 0:RT], pmu[:, 0:nh, :], ALU.add)

                    hsl = HT[:, :, q0:q0 + RT]
                    wt = wload(w1024(w_in_d, l, 1536, 256), 8, 256)
                    pa = nps()
                    for j in range(2):
                        for kc in range(8):
                            P.mm(pa[0:64, j * RT:(j + 1) * RT], wt[:, kc, j * 64:(j + 1) * 64], hsl[:, kc, :], start=(kc == 0), stop=(kc == 7))
                    WA = P.sb(ph2, [64, 2, RT], F32, "WA")
                    shift4(pa[0:64, 0:2 * RT].re("p (h t) -> p h t", h=2), 2, 24, 48, WA.v)
                    P.act(WL.v, WA[:, 0, :], AF.Tanh)
                    P.cp(AL.v, WA[:, 1, :])
                    pg = nps()
                    for kc in range(8):
                        P.mm(pg[:, 0:RT], wt[:, kc, 128:256], hsl[:, kc, :], start=(kc == 0), stop=(kc == 7))
                    P.cp(GL[:, 0:1], PREV[:, 26:27])
                    P.cp(GL[:, 1:RT + 1], pg[:, 0:RT], eng="act")
                    P.cp(PREV[:, 26:27], GL[:, RT:RT + 1])
                    P.ts(GS.v, GL[:, 0:RT], PV[:, pv0 + 50:pv0 + 51], ALU.mult)
                    P.stt(GS.v, GL[:, 1:RT + 1], OMM[:, l * 32 + 26:l * 32 + 27], GS.v, ALU.mult, ALU.add)
                    P.act(GSb.v, GS.v, AF.Sigmoid)
                    for hq in range(2):
                        p1 = nps()
                        p2 = nps()
                        p3 = nps()
                        for hh in range(4):
                            h = hq * 4 + hh
                            P.mm(p1[0:64, hh * RT:(hh + 1) * RT], SMb[0:64, h * 64:(h + 1) * 64], WL.v)
                            P.mm(p2[0:64, hh * RT:(hh + 1) * RT], SMb[0:64, 512 + h * 64:512 + (h + 1) * 64], AL.v)
                            P.mm(p3[0:64, hh * RT:(hh + 1) * RT], SMb[:, 1024 + h * 64:1024 + (h + 1) * 64], GSb.v)
                        for hh in range(4):
                            h = hq * 4 + hh
                            P.act(SG[:, h, :], p1[0:64, hh * RT:(hh + 1) * RT], AF.Sigmoid, bias=PV[0:64, pv0 + 51 + h:pv0 + 52 + h])
                            P.act(AA[:, h, :], p2[0:64, hh * RT:(hh + 1) * RT], AF.Sigmoid, bias=PV[0:64, pv0 + 59 + h:pv0 + 60 + h])
                        P.cp(GG[:, hq * 4:hq * 4 + 4, :], p3[0:64, 0:4 * RT].re("p (h t) -> p h t", h=4))
                    P.scan(CU.v.re("p h t -> p (h t)"), rmask8, SG.v.re("p h t -> p (h t)"), 0.0, ALU.mult, ALU.add)
                    P.act(E1.v, CU.v, AF.Exp, scale=-C0)
                    P.act(E2.v, CU.v, AF.Exp, scale=C0)
                    P.tt(SG.v, CU.v, SG.v, ALU.subtract)
                    P.act(E3.v, SG.v, AF.Exp, scale=-C0)
                    P.cp(GC.v, E1.v.re("p h (c t) -> p h c t", c=2)[:, :, :, C - 1])

                    P.mark('rw_lora')
                    def proj_heads(col0, prev0, mu0, dst):
                        for hq in range(2):
                            wt = wload(w1024(w_in_d, l, col0 + hq * 256, 256), 8, 256)
                            pb = nps()
                            for hh in range(4):
                                for kc in range(8):
                                    P.mm(pb[0:64, hh * RT:(hh + 1) * RT], wt[:, kc, hh * 64:(hh + 1) * 64], hsl[:, kc, :], start=(kc == 0), stop=(kc == 7))
                            shift4(pb[0:64, 0:4 * RT].re("p (h t) -> p h t", h=4), 4, prev0 + hq * 4, mu0 + hq * 4, dst[:, hq * 4:hq * 4 + 4, :])

                    proj_heads(0, 0, 24, RR.v)
                    KK = CU
                    proj_heads(512, 8, 32, KK.v)
                    proj_heads(1024, 16, 40, VV.v)

                    P.mark('rw_proj')

                    def pbc(col):
                        return PV[0:64, pv0 + col:pv0 + col + 8].re("p (h o) -> p h o", o=1).bc([64, 8, RT])

                    T1 = SG
                    T2 = P.sb(ph2, [64, 8, RT], F32, "T2")
                    P.tt(T1.v, KK.v, pbc(67), ALU.mult)
                    P.act(VVb.v, T1.v, AF.Square)
                    for hq in range(2):
                        pn = nps()
                        P.mm(pn[0:64, 0:4 * RT], ones_b[0:64, 0:64], VVb[:, hq * 4:hq * 4 + 4, :].re("p h t -> p (h t)"))
                        P.act(T2[:, hq * 4:hq * 4 + 4, :].re("p h t -> p (h t)"), pn[0:64, 0:4 * RT], AF.Ln, bias=EPS18[0:64, :])
                    P.act(T2.v, T2.v, AF.Exp, scale=-0.5)
                    P.tt(T1.v, T1.v, T2.v, ALU.mult)
                    P.stt(ATb.v, T1.v, -1.0, E3.v, ALU.mult, ALU.mult)
                    P.tt(T1.v, T1.v, AA.v, ALU.mult)
                    P.ts(T2.v, AA.v, -1.0, ALU.add)
                    P.tt(T2.v, T2.v, pbc(75), ALU.mult)
                    P.stt(T2.v, T2.v, 1.0, KK.v, ALU.add, ALU.mult)
                    P.tt(AA.v, RR.v, T2.v, ALU.mult)
                    P.tt(RTb.v, AA.v, pbc(83), ALU.mult)
                    for hq in range(2):
                        pn = nps()
                        P.mm(pn[0:64, 0:4 * RT], ones_b[0:64, 0:64], RTb[:, hq * 4:hq * 4 + 4, :].re("p h t -> p (h t)"))
                        P.cp(EV[:, hq * 4:hq * 4 + 4, :].re("p h t -> p (h t)"), pn[0:64, 0:4 * RT], eng="act")
                    P.tt(KTb.v, T2.v, E2.v, ALU.mult)
                    P.tt(BTb.v, T1.v, E2.v, ALU.mult)
                    P.tt(RTb.v, RR.v, E1.v, ALU.mult)
                    KTt, BTt, ATt, RTt = KTb, BTb, ATb, RTb
                    vfv = vf_d.v.re("h p t -> p h t")[:, :, g0:g0 + RT]
                    if l == 0:
                        P.dma("sp", vfv, VV.v)
                    else:
                        VF = RR
                        P.dma("sp", VF.v, vfv)
                        p32 = nps()
                        for h in range(8):
                            P.mm(p32[0:32, 0:RT], SM[0:64, 1536 + h * 32:1536 + (h + 1) * 32], VV[:, h, :], start=(h == 0), stop=(h == 7))
                        T32 = P.sb(ph2, [32, RT], F32, "T32")
                        P.cp(T32.v, p32[0:32, 0:RT], eng="act")
                        for hq in range(2):
                            pz = nps()
                            for hh in range(4):
                                h = hq * 4 + hh
                                P.mm(pz[0:64, hh * RT:(hh + 1) * RT], SM[0:32, 1792 + h * 64:1792 + (h + 1) * 64], T32.v)
                            for hh in range(4):
                                h = hq * 4 + hh
                                P.act(T2[:, h, :], pz[0:64, hh * RT:(hh + 1) * RT], AF.Sigmoid, bias=PV[0:64, pv0 + 107 + h:pv0 + 108 + h])
                        P.tt(T1.v, VF.v, VV.v, ALU.subtract)
                        P.tt(T1.v, T1.v, T2.v, ALU.mult)
                        P.tt(VV.v, VV.v, T1.v, ALU.add)
                    P.tt(EV.v, EV.v, VV.v, ALU.mult)
                    P.mark('rw_prep')
                    P.cp(VVb.v, VV.v, eng="act")
                    for (src, dstk) in ((VVb, VTK), (BTt, BTK), (KTt, KTK)):
                        for c in range(2):
                            pt = nps(8)
                            ptb = pt.v.cast(BF16)
                            for h in range(8):
                                P.tr(ptb[0:64, h * 64:(h + 1) * 64], src[:, h, c * C:(c + 1) * C], ident_b[0:64, 0:64])
                            P.cp(dstk[:, c, :], ptb[0:64, 0:512], eng=("act" if c else "dve"))
                    P.barrier()
                    ph2.close()
                    P.mark('rw_tr')
                    def t64(nm, dt=BF16):
                        return P.sb(ph, [64, 8, 64], dt, nm)
                    CH = []
                    for c in range(2):
                        CH.append(dict(NM=t64("NM"), LM=t64("LM"), N2=t64("N2"), L2=t64("L2"), MAK=t64("MAK"), MRB=t64("MRB"),
                                       MRK=t64("MRK"), PP=t64("PP", F32), PPb=t64("PPb")))
                    WS, US = t64("WS"), t64("US")
                    YSc = [t64("YS", F32) for _ in range(2)]
                    YQc = [t64("YQ", F32) for _ in range(2)]
                    MEc = [t64("MEAN", F32) for _ in range(2)]

                    def gram(dst, a, b, c, mask):
                        pgm = nps(8)
                        for h in range(8):
                            P.mm(pgm[0:64, h * 64:(h + 1) * 64], a[:, h, c * C:(c + 1) * C], b[:, h, c * C:(c + 1) * C])
                        P.tt(dst.v, pgm[0:64, :].re("p (h t) -> p h t", h=8), m8(mask), ALU.mult)

                    def mm8(dst_ps, a, b):
                        for h in range(8):
                            P.mm(dst_ps[0:64, h * 64:(h + 1) * 64], a[:, h, :], b[:, h, :])

                    def v8(ps):
                        return ps[0:64, :].re("p (h t) -> p h t", h=8)

                    for c in range(2):
                        d = CH[c]
                        gram(d["NM"], BTt, ATt, c, su)
                        gram(d["LM"], ATt, BTt, c, sl)
                        gram(d["MAK"], KTt, ATt, c, su)
                        gram(d["MRB"], BTt, RTt, c, iu)
                        gram(d["MRK"], KTt, RTt, c, iu)
                        P.tt(d["PP"].v, d["NM"].v, m8(id64), ALU.add)
                        P.cp(d["PPb"].v, d["PP"].v, eng="act")
                        d["cur"] = (d["LM"], d["NM"], d["L2"], d["N2"])
                    for lev in range(1, 6):
                        pls, pns, pqs = [], [], []
                        for c in range(2):
                            Lk, Nk, Lo, No = CH[c]["cur"]
                            pl = nps(8)
                            mm8(pl, Nk, Lk)
                            pls.append(pl)
                            if lev < 5:
                                pn = nps(8)
                                mm8(pn, Lk, Nk)
                                pns.append(pn)
                        for c in range(2):
                            Lk, Nk, Lo, No = CH[c]["cur"]
                            P.cp(Lo.v, v8(pls[c]), eng="act")
                            if lev < 5:
                                P.cp(No.v, v8(pns[c]))
                            CH[c]["cur"] = (Lo, No, Lk, Nk)
                        for c in range(2):
                            Lk = CH[c]["cur"][0]
                            pq = nps(8)
                            mm8(pq, Lk, CH[c]["PPb"])
                            pqs.append(pq)
                        for c in range(2):
                            P.tt(CH[c]["PP"].v, CH[c]["PP"].v, v8(pqs[c]), ALU.add)
                            P.cp(CH[c]["PPb"].v, CH[c]["PP"].v, eng="act")
                    P.mark('rw_inv')
                    for c in range(2):
                        d = CH[c]
                        MAK, MRB, MRK, PPb = d["MAK"], d["MRB"], d["MRK"], d["PPb"]
                        pw = nps(8)
                        for h in range(8):
                            P.mm(pw[0:64, h * 64:(h + 1) * 64], ATt[:, h, c * C:(c + 1) * C], Hb[:, h, :], start=True, stop=False)
                            P.mm(pw[0:64, h * 64:(h + 1) * 64], MAK[:, h, :], VTK[:, c, h * 64:(h + 1) * 64], start=False, stop=True)
                        P.cp(WS.v, v8(pw), eng="act")
                        pu = nps(8)
                        mm8(pu, PPb, WS)
                        P.cp(US.v, v8(pu), eng="act")
                        py = nps(8)
                        for h in range(8):
                            o = py[0:64, h * 64:(h + 1) * 64]
                            P.mm(o, Hb[:, h, :], RTt[:, h, c * C:(c + 1) * C], start=True, stop=False)
                            P.mm(o, US[:, h, :], MRB[:, h, :], start=False, stop=False)
                            P.mm(o, VTK[:, c, h * 64:(h + 1) * 64], MRK[:, h, :], start=False, stop=True)
                        phh = nps(8)
                        for h in range(8):
                            o = phh[0:64, h * 64:(h + 1) * 64]
                            P.mm(o, BTK[:, c, h * 64:(h + 1) * 64], US[:, h, :], start=True, stop=False)
                            P.mm(o, KTK[:, c, h * 64:(h + 1) * 64], VTK[:, c, h * 64:(h + 1) * 64], start=False, stop=True)
                        P.tt(Hst.v, Hst.v, v8(phh), ALU.add)
                        P.tt(Hst.v, Hst.v, GC[:, :, c:c + 1].bc([64, 8, 64]), ALU.mult)
                        P.cp(Hb.v, Hst.v, eng="act")
                        P.mark('rw_state')
                        P.cp(YSc[c].v, v8(py), eng="act")
                    fl = "p h t -> p (h t)"
                    pcs, pqs2 = [], []
                    for c in range(2):
                        pc = nps(8)
                        P.mm(pc[0:64, :], Cm, YSc[c].v.re(fl))
                        pcs.append(pc)
                    for c in range(2):
                        P.act(YQc[c].v.re(fl), pcs[c][0:64, :], AF.Square)
                    for c in range(2):
                        pq2 = nps(8)
                        P.mm(pq2[0:64, :], ones[0:64, 0:64], YQc[c].v.re(fl))
                        pqs2.append(pq2)
                    for c in range(2):
                        P.act(MEc[c].v.re(fl), pqs2[c][0:64, :], AF.Ln, bias=EPSLN[0:64, :], scale=1.0 / 64)
                    for c in range(2):
                        P.act(MEc[c].v, MEc[c].v, AF.Exp, scale=-0.5)
                    for c in range(2):
                        P.tt(YSc[c].v.re(fl), pcs[c][0:64, :], MEc[c].v.re(fl), ALU.mult)
                    lg = PV[0:64, pv0 + 91:pv0 + 99].re("p (h o) -> p h o", o=1).bc([64, 8, 64])
                    lb = PV[0:64, pv0 + 99:pv0 + 107].re("p (h o) -> p h o", o=1).bc([64, 8, 64])
                    for c in range(2):
                        P.tt(YSc[c].v, YSc[c].v, lg, ALU.mult)
                    for c in range(2):
                        P.tt(YSc[c].v, YSc[c].v, lb, ALU.add)
                    for c in range(2):
                        P.tt(YSc[c].v, YSc[c].v, EV[:, :, c * C:(c + 1) * C], ALU.add)
                    for c in range(2):
                        P.tt(YA[:, :, q0 + c * C:q0 + (c + 1) * C], YSc[c].v, GG[:, :, c * C:(c + 1) * C], ALU.mult)
                    if True:
                        P.mark('rw_epi')
                    P.barrier()
            if l == 0 and ti == 0:
                dump("ya", YA.v)

            with contextlib.ExitStack() as ph:
                QT = P.sb(ph, [128, 4, TT], BF16, "QT")
                QR = [P.sb(ph, [128, TT], F32, "QR") for _ in range(2)]
                SQ = [P.sb(ph, [128, TT], BF16, "SQs") for _ in range(2)]
                RS = [P.sb(ph, [128, TT], F32, "RSs") for _ in range(2)]
                for hp2 in range(2):
                    wqk = [wload(w1024(w_in_d, l, 1792 + which * 512 + hp2 * 256, 256), 8, 256) for which in range(2)]
                    for pp in range(2):
                        p = hp2 * 2 + pp
                        W2 = range(2)
                        pqs, sss = [], []
                        for which in W2:
                            pq = nps()
                            pqs.append(pq)
                            for kc in range(8):
                                P.mm(pq.v, wqk[which][:, kc, pp * 128:(pp + 1) * 128], HT[:, kc, :], start=(kc == 0), stop=(kc == 7))
                        for which in W2:
                            P.cp(QR[which].v, pqs[which].v, eng="act")
                        for which in W2:
                            P.act(SQ[which].v, QR[which].v, AF.Square)
                        for which in W2:
                            ss = nps()
                            sss.append(ss)
                            P.mm(ss.v, bones_b, SQ[which].v)
                        P.act(RS[0].v, sss[0].v, AF.Ln, bias=EPS64, scale=1.0)
                        P.act(RS[1].v, sss[1].v, AF.Ln, bias=EPS6, scale=1.0 / 64)
                        for which in W2:
                            P.act(RS[which].v, RS[which].v, AF.Exp, scale=-0.5)
                        for which in W2:
                            dst = QT[:, p, :] if which == 0 else KT[:, p, t0:t0 + TT]
                            P.stt(dst, QR[which].v, PV[:, pv0 + 115 + which:pv0 + 116 + which], RS[which].v, ALU.mult, ALU.mult)
                for vp in range(2):
                    wt = wload(w1024(w_in_d, l, 1792 + 1024 + vp * 256, 256), 8, 256)
                    for tb in range(4):
                        pvv = nps()
                        for kc in range(8):
                            P.mm(pvv[:, 0:256], HT[:, kc, tb * 128:(tb + 1) * 128], wt[:, kc, :], start=(kc == 0), stop=(kc == 7))
                        P.cp(VC[:, ti * 4 + tb, vp * 256:(vp + 1) * 256], pvv[:, 0:256], eng=("act" if tb % 2 else "dve"))
                G = 4
                EZ = [P.sb(ph, [128, TT], F32, "EZ") for _ in range(G)]
                SP = [[P.sb(ph, [128, TT], BF16, "SP") for _ in range(G)] for _ in range(2)]
                ATb = [P.sb(ph, [128, TT], BF16, "ATb") for _ in range(G)]
                RRW = [P.sb(ph, [128, TT], F32, "RRW") for _ in range(G)]
                RHI = [P.sb(ph, [128, TT], BF16, "RHI") for _ in range(G)]
                nb = (t0 + TT) // 128
                zc = [0, 0]

                def bank(kind):
                    i = zc[kind] % 2
                    zc[kind] += 1
                    return PS[kind * 2 + i]

                def geom(kb):
                    bl = kb - ti * 4
                    cs = max(0, bl) * 128
                    return bl, cs, TT - cs

                def group(ps, mms):
                    n = len(mms)
                    for j, (o, a_, b_) in enumerate(mms):
                        P.mm(o, a_, b_, start=(j == 0), stop=(j == n - 1))

                def stage_a(grp, kb, slot):
                    bl, cs, ncol = geom(kb)
                    for i in range(G):
                        h = grp * G + i
                        p, b = h // 2, (h % 2) * 64
                        pz = bank(0)
                        mms = [(pz[:, 0:ncol], KT[b:b + 64, p, kb * 128:(kb + 1) * 128], QT[b:b + 64, p, cs:TT])]
                        if bl >= 0:
                            mms.append((pz[:, 0:128], ident_b, negm_b))
                        group(pz, mms)
                        P.act(EZ[i][:, 0:ncol], pz[:, 0:ncol], AF.Exp)
                    for i in range(G):
                        P.act(SP[slot][i][:, 0:ncol], EZ[i][:, 0:ncol], AF.Ln, bias=ONE)

                def stage_b(grp, kb, slot):
                    bl, cs, ncol = geom(kb)
                    first = (kb == nb - 1)
                    for i in range(G):
                        h = grp * G + i
                        p, b = h // 2, (h % 2) * 64
                        pzb = bank(1)
                        mms = [(pzb[:, 0:ncol], KT[b:b + 64, p, kb * 128:(kb + 1) * 128], QT[b:b + 64, p, cs:TT]),
                               (pzb[:, 0:ncol], ntri_b, SP[slot][i][:, 0:ncol])]
                        if not first:
                            mms.append((pzb[:, 0:ncol], nones_b, RHI[i][:, cs:TT]))
                        if bl >= 0:
                            mms.append((pzb[:, 0:128], ident_b, negm_b))
                        group(pzb, mms)
                        P.act(ATb[i][:, cs:TT], pzb[:, 0:ncol], AF.Exp)
                    if kb > 0:
                        for i in range(G):
                            P.tt(RRW[i][:, cs:TT], RRW[i][:, cs:TT], SP[slot][i][:, 0:ncol], ALU.add)
                            P.cp(RHI[i].v, RRW[i].v, eng=("act" if i == 3 else "dve"))
                    for i in range(G):
                        h = grp * G + i
                        P.mm(PS[4 + i][0:64, :], VC[:, kb, h * 64:(h + 1) * 64], ATb[i].v, start=first, stop=(kb == 0))

                for grp in range(8 // G):
                    for i in range(G):
                        P.memset(RRW[i].v, 0.0)
                        P.memset(RHI[i].v, 0.0, eng="pool")
                        P.memset(ATb[i].v, 0.0, eng="pool")
                    kbs = list(range(nb - 1, -1, -1))
                    stage_a(grp, kbs[0], 0)
                    for k, kb in enumerate(kbs):
                        if k + 1 < len(kbs):
                            stage_a(grp, kbs[k + 1], (k + 1) % 2)
                        stage_b(grp, kb, k % 2)
                    for i in range(G):
                        h = grp * G + i
                        P.cp(YB[:, h, :], PS[4 + i][0:64, :], eng=("act" if i % 2 else "dve"))
                P.barrier()
            if l == 0 and ti == 0:
                dump("yb", YB.v)

            with contextlib.ExitStack() as ph:
                QR = [P.sb(ph, [128, TT], F32, "QRm") for _ in range(2)]
                SQ = [P.sb(ph, [128, TT], BF16, "SQm2") for _ in range(2)]
                RS = [P.sb(ph, [128, TT], F32, "RSm2") for _ in range(2)]
                QM = [P.sb(ph, [128, TT], BF16, "QM") for _ in range(2)]
                ES = [[P.sb(ph, [128, TT], BF16, "ES") for _ in range(2)] for _ in range(2)]
                for hp2 in range(2):
                    wt = wload(w1024(w_in_d, l, 3328 + hp2 * 256, 256), 8, 256)
                    R2 = range(2)
                    pqs = []
                    for pp in R2:
                        pq = nps()
                        pqs.append(pq)
                        for kc in range(8):
                            P.mm(pq.v, wt[:, kc, pp * 128:(pp + 1) * 128], HT[:, kc, :], start=(kc == 0), stop=(kc == 7))
                    for pp in R2:
                        P.cp(QR[pp].v, pqs[pp].v, eng="act")
                    for pp in R2:
                        P.act(SQ[pp].v, QR[pp].v, AF.Square)
                    sss = []
                    for pp in R2:
                        ss = nps()
                        sss.append(ss)
                        P.mm(ss.v, ones_b, SQ[pp].v)
                    for pp in R2:
                        P.act(RS[pp].v, sss[pp].v, AF.Ln, bias=EPS128, scale=1.0)
                    for pp in R2:
                        P.act(RS[pp].v, RS[pp].v, AF.Exp, scale=-0.5)
                    for pp in R2:
                        P.stt(QM[pp].v, QR[pp].v, PV[:, pv0 + 117:pv0 + 118], RS[pp].v, ALU.mult, ALU.mult)
                    for mb in range(2):
                        psl = []
                        for pp in R2:
                            h = hp2 * 2 + pp
                            pss = nps()
                            psl.append(pss)
                            P.mm(pss.v, MK[:, h, mb * 128:(mb + 1) * 128], QM[pp].v)
                        for pp in R2:
                            P.act(ES[pp][mb].v, psl[pp].v, AF.Exp)
                    pds, pos = [], []
                    for pp in R2:
                        h = hp2 * 2 + pp
                        pd = nps()
                        pds.append(pd)
                        P.mm(pd.v, ones_b, ES[pp][0].v, start=True, stop=False)
                        P.mm(pd.v, ones_b, ES[pp][1].v, start=False, stop=True)
                        po = nps()
                        pos.append(po)
                        P.mm(po.v, MV[:, 0, h * 128:(h + 1) * 128], ES[pp][0].v, start=True, stop=False)
                        P.mm(po.v, MV[:, 1, h * 128:(h + 1) * 128], ES[pp][1].v, start=False, stop=True)
                    for pp in R2:
                        P.act(RS[pp].v, pds[pp].v, AF.Ln)
                    for pp in R2:
                        P.act(RS[pp].v, RS[pp].v, AF.Exp, scale=-1.0)
                    for pp in R2:
                        P.tt(YM[:, hp2 * 2 + pp, :], pos[pp].v, RS[pp].v, ALU.mult)
                P.barrier()
            if l == 0 and ti == 0:
                dump("ym", YM.v)

            with contextlib.ExitStack() as ph:
                MG = P.sb(ph, [128, 8, TT], BF16, "MG")
                SGm = [P.sb(ph, [128, TT], F32, "SGm") for _ in range(3)]
                WB = [[P.sb(ph, [128, 8, 256], BF16, f"WB{i}") for i in range(3)] for _ in range(2)]
                AC = [[P.sb(ph, [128, TT], F32, "AC") for _ in range(3)] for _ in range(2)]
                for d2 in range(4):
                    wb = WB[d2 % 2]
                    for b in range(2):
                        P.dma("pool", wb[b][0:64, :, :], w_br01_d[l][b][d2].re("p (h n) -> p h n", n=256))
                    P.dma("pool", wb[2][:, 0:4, :], w_br2_d[l][d2].re("p (c n) -> p c n", n=256))
                    gw = [wload(w1024(w_in_d, l, 3840 + b * 1024 + d2 * 256, 256), 8, 256) for b in range(3)]
                    for dd in range(2):
                        d = d2 * 2 + dd
                        ac = AC[d % 2]
                        cols = slice(dd * 128, (dd + 1) * 128)
                        for b in range(3):
                            pg = nps()
                            for kc in range(8):
                                P.mm(pg.v, gw[b][:, kc, cols], HT[:, kc, :], start=(kc == 0), stop=(kc == 7))
                            P.act(SGm[b].v, pg.v, AF.Sigmoid)
                            pb = nps()
                            if b < 2:
                                ysrc = YA if b == 0 else YB
                                for h in range(8):
                                    P.mm(pb.v, wb[b][0:64, h, cols], ysrc[:, h, :], start=(h == 0), stop=(h == 7))
                            else:
                                for c4 in range(4):
                                    P.mm(pb.v, wb[2][:, c4, cols], YM[:, c4, :], start=(c4 == 0), stop=(c4 == 3))
                            P.tt(ac[b].v, SGm[b].v, pb.v, ALU.mult)
                        P.tt(ac[0].v, ac[0].v, ac[1].v, ALU.add)
                        P.tt(MG[:, d, :], ac[0].v, ac[2].v, ALU.add)
                if l == 0 and ti == 0:
                    dump("merged", MG.v)
                for d2 in range(4):
                    wt = wload(w1024(w_out_d, l, d2 * 256, 256), 8, 256)
                    for dd in range(2):
                        d = d2 * 2 + dd
                        px = nps()
                        for kc in range(8):
                            P.mm(px.v, wt[:, kc, dd * 128:(dd + 1) * 128], MG[:, kc, :], start=(kc == 0), stop=(kc == 7))
                        P.tt(XT[:, d, :], XT[:, d, :], px.v, ALU.add)
                P.barrier()
            if l == 0 and ti == 0:
                dump("xattn", XT.v)

            with contextlib.ExitStack() as ph:
                rmsnorm(ph, XT, 8, pv0 + 8, HT, TT)
                ACTB = P.sb(ph, [128, 22, TT], BF16, "ACTB")
                WD = [P.sb(ph, [128, 22, 128], BF16, f"WD{i}") for i in range(3)]
                HB = [P.sb(ph, [128, TT + 2], BF16, "HB") for _ in range(4)]
                DG = [P.sb(ph, [128, 3, 128], BF16, "DG") for _ in range(4)]
                ACg = [P.sb(ph, [128, TT], F32, "ACg") for _ in range(3)]
                ACv = [P.sb(ph, [128, TT], F32, "ACv") for _ in range(3)]
                specs = []
                for c2 in range(11):
                    for dd in range(2):
                        specs.append(("g", c2, dd))
                        specs.append(("v", c2, dd))
                wcur = {}
                stt_ = {}

                def ffn_up(k):
                    kind, c2, dd = specs[k]
                    c = c2 * 2 + dd
                    ci = c if kind == "g" else 22 + c
                    if dd == 0:
                        wcur[kind] = wload(w1024(w_up_d, l, (0 if kind == "g" else DFF) + c2 * 256, 256), 8, 256)
                    wt = wcur[kind]
                    pu = nps()
                    for kc in range(8):
                        P.mm(pu.v, wt[:, kc, dd * 128:(dd + 1) * 128], HT[:, kc, :], start=(kc == 0), stop=(kc == 7))
                    hb = HB[k % 4]
                    dg = DG[k % 4]
                    for j in range(3):
                        wcol = PV[:, pv0 + 163 + 44 * j + ci:pv0 + 164 + 44 * j + ci]
                        P.act(dg[:, j, :], ident, AF.Identity, scale=wcol)
                    P.cp(hb[:, 0:2], HALO[:, ci, :])
                    P.cp(hb[:, 2:TT + 2], pu.v, eng="act")
                    P.cp(HALO[:, ci, :], hb[:, TT:TT + 2])
                    stt_[k] = (hb, dg, ci, c, kind)

                def ffn_conv(k):
                    hb, dg, ci, c, kind = stt_.pop(k)
                    acc = (ACg if kind == "g" else ACv)[c % 3]
                    pc = nps()
                    for j in range(3):
                        P.mm(pc.v, dg[:, j, :], hb[:, j:j + TT], start=(j == 0), stop=(j == 2))
                    P.act(acc.v, pc.v, AF.Silu if kind == "g" else AF.Identity, bias=PV[:, pv0 + 119 + ci:pv0 + 120 + ci])
                    if kind == "v":
                        P.tt(ACTB[:, c, :], ACg[c % 3].v, ACv[c % 3].v, ALU.mult)

                ffn_up(0)
                for k in range(len(specs)):
                    if k + 1 < len(specs):
                        ffn_up(k + 1)
                    ffn_conv(k)
                for d in range(8):
                    wd = WD[st["wd"] % 3]
                    st["wd"] += 1
                    src = w_dn_d[l][d].re("p (c n) -> p c n", n=128)
                    for a, bnd in ((0, 22),):
                        P.dma("pool", wd[:, a:bnd, :], src[:, a:bnd, :])
                    px = nps()
                    for c in range(22):
                        P.mm(px.v, wd[:, c, :], ACTB[:, c, :], start=(c == 0), stop=(c == 21))
                    P.tt(XT[:, d, :], XT[:, d, :], px.v, ALU.add)
                for a in range(0, 8, 4):
                    P.dma("sp", xdst.v.re("(c p) n -> p c n", p=128)[:, a:a + 4, t0:t0 + TT], XT[:, a:a + 4, :],
                          semtile=XT, is_output=last)
                P.barrier()
    P.emit()
    root.close()
    nc._marks = P.marks
    return nc


def _consts():
    c = np.zeros((128, NCONST), np.float32)
    c[:, 0:128] = np.eye(128)
    c[:, 128:256] = 1.0
    bo = np.zeros((128, 128), np.float32)
    bo[0:64, 0:64] = 1.0
    bo[64:128, 64:128] = 1.0
    c[:, 256:384] = bo
    s = np.arange(64)[:, None]
    t = np.arange(64)[None, :]
    c[0:64, 384:448] = (s < t)
    c[0:64, 448:512] = (s <= t)
    c[0:64, 512:576] = (s > t)
    c[0:64, 576:640] = (s == t)
    s = np.arange(128)[:, None]
    t = np.arange(128)[None, :]
    c[:, 640:768] = (s < t)
    rm = np.ones((64, RT), np.float32)
    rm[:, 0::C] = 0.0
    c[0:64, 768:768 + RT] = rm
    c[:, 896:1024] = (s >= t)
    c[:, 1024] = 1e-6
    c[:, 1025] = 64e-6
    c[:, 1026] = 128e-6
    c[:, 1027] = 64e-5
    c[:, 1028] = 1.0
    c[:, 1029] = 1e-18
    c[0, 1152:1280] = 1.0
    c[:, 1280:1408] = -(s >= t).astype(np.float32)
    c[:, 1408:1536] = -1.0
    c[:, 1536:1664] = -30000.0 * (s >= t)
    c[0:64, 1664:1728] = np.eye(64) - 1.0 / 64
    return c


def _pack(inp):
    pv = np.zeros((128, DEPTH * PVS), np.float32)
    sm = np.zeros((DEPTH, 128, SMW), np.float32)
    for l in range(DEPTH):
        o = l * PVS

        def fm(v, n):
            return np.asarray(v, np.float32).reshape(n, 128).T

        def hm(v):
            return np.asarray(v, np.float32).reshape(8, 64).T

        pv[:, o + 0:o + 8] = fm(inp["norm1_g"][l], 8)
        pv[:, o + 8:o + 16] = fm(inp["norm2_g"][l], 8)
        pv[:, o + 16:o + 24] = fm(inp["mem_norm_g"][l], 8)
        mu = np.asarray(inp["shift_mu"][l], np.float32)
        pv[0:64, o + 24:o + 32] = hm(mu[0:512])
        pv[0:64, o + 32:o + 40] = hm(mu[512:1024])
        pv[0:64, o + 40:o + 48] = hm(mu[1024:1536])
        pv[0:64, o + 48] = mu[1536:1600]
        pv[0:64, o + 49] = mu[1600:1664]
        pv[:, o + 50] = mu[1664:1792]
        pv[0:64, o + 51:o + 59] = hm(inp["decay_w0"][l])
        pv[0:64, o + 59:o + 67] = hm(inp["iclr_a0"][l])
        pv[0:64, o + 67:o + 75] = hm(inp["k_k"][l])
        pv[0:64, o + 75:o + 83] = hm(inp["k_a"][l])
        pv[0:64, o + 83:o + 91] = hm(np.asarray(inp["r_k"][l]).reshape(-1))
        pv[0:64, o + 91:o + 99] = hm(inp["lnx_g"][l])
        pv[0:64, o + 99:o + 107] = hm(inp["lnx_b"][l])
        if l > 0:
            pv[0:64, o + 107:o + 115] = hm(inp["vres_v0"][l - 1])
        pv[:, o + 115] = np.tile(np.asarray(inp["sb_q_norm_g"][l], np.float32), 2)
        pv[:, o + 116] = np.tile(np.asarray(inp["sb_k_norm_g"][l], np.float32), 2)
        pv[:, o + 117] = inp["mem_q_norm_g"][l]
        pv[:, o + 118] = inp["mem_k_norm_g"][l]
        pv[:, o + 119:o + 163] = fm(inp["conv_b"][l], 44)
        cw = np.asarray(inp["conv_w"][l], np.float32)
        pv[:, o + 163:o + 207] = fm(cw[0], 44)
        pv[:, o + 207:o + 251] = fm(cw[1], 44)
        pv[:, o + 251:o + 295] = fm(cw[2], 44)
        sm[l, 0:64, 0:512] = inp["decay_w2"][l]
        sm[l, 0:64, 512:1024] = inp["iclr_a2"][l]
        sm[l, :, 1024:1536] = inp["gate_g2"][l]
        if l > 0:
            v1 = np.asarray(inp["vres_v1"][l - 1], np.float32)
            sm[l, 0:64, 1536:1536 + 256] = v1.reshape(8, 64, 32).transpose(1, 0, 2).reshape(64, 256)
    return pv, sm


def _pack2(inp, sm):
    for l in range(1, DEPTH):
        sm[l, 0:32, 1792:1792 + 512] = inp["vres_v2"][l - 1]
    return sm


def _wdown(w):
    w = np.asarray(w, np.float32).reshape(DEPTH, 22, 128, 8, 128).transpose(0, 3, 2, 1, 4)
    return np.ascontiguousarray(w).reshape(DEPTH, 8, 128, 22 * 128)


def _tile256(w):
    w = np.asarray(w, np.float32)
    L, K, N = w.shape
    w = w.reshape(L, K // 128, 128, N // 256, 256).transpose(0, 3, 2, 1, 4)
    return np.ascontiguousarray(w).reshape(L, N // 256, 128, (K // 128) * 256)


def _weights(inp):
    wb = np.asarray(inp["w_branch"], np.float32)
    b01 = wb[:, 0:2].reshape(DEPTH, 2, 8, 64, 4, 256).transpose(0, 1, 4, 3, 2, 5)
    b2 = wb[:, 2].reshape(DEPTH, 4, 128, 4, 256).transpose(0, 3, 2, 1, 4)
    return {
        "w_in": _tile256(inp["w_in"]),
        "w_mem_kv": _tile256(inp["w_mem_kv"]),
        "w_out": _tile256(inp["w_out"]),
        "w_up": _tile256(inp["w_up"]),
        "w_br01": np.ascontiguousarray(b01).reshape(DEPTH, 2, 4, 64, 2048),
        "w_br2": np.ascontiguousarray(b2).reshape(DEPTH, 4, 128, 1024),
        "w_down": _wdown(inp["w_down"]),
    }


_CACHE = {}


def kernel(**inputs):
    inp = {k: np.asarray(v) for k, v in inputs.items()}
    x = inp["x"].astype(np.float32, copy=False)
    mem = inp["mem"].astype(np.float32, copy=False)
    pv, sm = _pack(inp)
    sm = _pack2(inp, sm)
    consts = _consts()
    if "nc" not in _CACHE:
        _CACHE["nc"] = build()
    nc = _CACHE["nc"]
    shared = {
        "pvec": pv, "smat": sm, "consts": consts,
        **_weights(inp),
    }
    in_maps = []
    for b in range(8):
        m = dict(shared)
        m["xT"] = np.ascontiguousarray(x[b].T)
        m["memT"] = np.ascontiguousarray(mem[b].T)
        in_maps.append(m)
    res = run_bass_kernel_spmd(nc, in_maps, core_ids=list(range(8)))
    out = np.stack([np.ascontiguousarray(res.results[b]["yT"].T) for b in range(8)], axis=0)
    return out.astype(np.float32)
```
